# Optimizing a Trainium2 kernel written in Bass

```python
import jax, jax.numpy as jnp
from jax import lax
import numpy as np

D_MODEL = 1024
BATCH = 16
SEQ = 2048
DEPTH = 4
DEC_BATCH = 4
DEC_SEQ = 4096
PAST_LEN = 128

N_MEM = 256
D_FF = 2816
D_CONV = D_MODEL // 2
CONV_W = 3
GLA_HEADS = 4
D_GLA_V = D_MODEL // 2
HEAD_V = D_GLA_V // GLA_HEADS
D_GLA_K = D_GLA_V // 2
HEAD_K = D_GLA_K // GLA_HEADS
GATE_RANK = 16
GATE_TEMP = 16.0
GATE_BIAS_MEAN = 2.0
CHUNK = 64
XA_HEADS = 4
XA_HEAD_DIM = D_MODEL // XA_HEADS
LN_EPS = 1e-5
RMS_EPS = 1e-6
DN_ALPHA = (2 * DEPTH) ** 0.25
DN_BETA = (8 * DEPTH) ** -0.25

IN_SPLITS = (D_CONV, D_CONV, D_CONV, D_GLA_K, D_GLA_K, D_GLA_V, D_GLA_V, GATE_RANK, GATE_RANK, D_MODEL, D_MODEL)
IN_OFFSETS = tuple(int(o) for o in np.cumsum(IN_SPLITS)[:-1])
D_IN = int(sum(IN_SPLITS))

kernel_name = 'hybrid_conv_gla_memory_encoder'


def layer_norm(x, g, b):
    xf = x.astype(jnp.float32)
    mu = jnp.mean(xf, axis=-1, keepdims=True)
    var = jnp.mean(jnp.square(xf - mu), axis=-1, keepdims=True)
    return ((xf - mu) * lax.rsqrt(var + LN_EPS) * g.astype(jnp.float32) + b.astype(jnp.float32)).astype(x.dtype)


def swiglu(x, w_gu, w_down):
    gate, up = jnp.split(x @ w_gu, 2, axis=-1)
    return (jax.nn.silu(gate) * up) @ w_down


def short_conv(u, w):
    up = jnp.pad(u, ((0, 0), (1, 1), (0, 0)))
    return up[:, :-2] * w[0] + up[:, 1:-1] * w[1] + up[:, 2:] * w[2]


def gla_direction(q, k, v, log_a, include_diag):
    B, T, H, DK = q.shape
    DV = v.shape[-1]
    N = T // CHUNK
    f32 = jnp.float32
    qc = q.astype(f32).reshape(B, N, CHUNK, H, DK)
    kc = k.astype(f32).reshape(B, N, CHUNK, H, DK)
    vc = v.astype(f32).reshape(B, N, CHUNK, H, DV)
    b = jnp.cumsum(log_a.astype(f32).reshape(B, N, CHUNK, H, DK), axis=2)
    ref = b[:, :, CHUNK // 2 - 1:CHUNK // 2]
    q_in = qc * jnp.exp(b - ref)
    k_in = kc * jnp.exp(ref - b)
    mask = jnp.tril(jnp.ones((CHUNK, CHUNK), dtype=bool), 0 if include_diag else -1)
    att = jnp.where(mask, jnp.einsum('bnihd,bnjhd->bnhij', q_in, k_in), 0.0)
    o = jnp.einsum('bnhij,bnjhe->bnihe', att, vc)
    b_last = b[:, :, -1]
    kv = jnp.einsum('bnjhd,bnjhe->bnhde', kc * jnp.exp(b_last[:, :, None] - b), vc)

    def step(S, inp):
        kv_n, d_n = inp
        return d_n[..., None] * S + kv_n, S

    S0 = jnp.zeros((B, H, DK, DV), f32)
    _, S_prev = lax.scan(step, S0, (jnp.moveaxis(kv, 1, 0), jnp.moveaxis(jnp.exp(b_last), 1, 0)))
    S_prev = jnp.moveaxis(S_prev, 0, 1)
    o = o + jnp.einsum('bnihd,bnhde->bnihe', qc * jnp.exp(b), S_prev)
    return o.reshape(B, T, H, DV)


def parallel_mixer(x, w_in, conv_w, w_conv_out, gla_w2, gla_b, gla_norm_g, w_gla_out, w_out):
    B, T, _ = x.shape
    h, gb, gc, q, k, v, g, lr_f, lr_b, m_conv, m_gla = jnp.split(x @ w_in, IN_OFFSETS, axis=-1)
    y_conv = (gb * short_conv(gc * h, conv_w)) @ w_conv_out
    q = q.reshape(B, T, GLA_HEADS, HEAD_K) * (HEAD_K ** -0.5)
    k = k.reshape(B, T, GLA_HEADS, HEAD_K)
    v = v.reshape(B, T, GLA_HEADS, HEAD_V)
    log_a_f = (jax.nn.log_sigmoid((lr_f @ gla_w2[0] + gla_b[0]).astype(jnp.float32)) / GATE_TEMP).reshape(B, T, GLA_HEADS, HEAD_K)
    log_a_b = (jax.nn.log_sigmoid((lr_b @ gla_w2[1] + gla_b[1]).astype(jnp.float32)) / GATE_TEMP).reshape(B, T, GLA_HEADS, HEAD_K)
    o_f = gla_direction(q, k, v, log_a_f, True)
    o_b = jnp.flip(gla_direction(jnp.flip(q, 1), jnp.flip(k, 1), jnp.flip(v, 1), jnp.flip(log_a_b, 1), False), 1)
    o = o_f + o_b
    o = o * lax.rsqrt(jnp.mean(jnp.square(o), axis=-1, keepdims=True) + RMS_EPS) * gla_norm_g.astype(jnp.float32)
    o = o.astype(x.dtype).reshape(B, T, D_GLA_V) * jax.nn.silu(g)
    y_gla = o @ w_gla_out
    merged = jax.nn.sigmoid(m_conv) * y_conv + jax.nn.sigmoid(m_gla) * y_gla
    return merged @ w_out


def memory_cross_attention(x, mem, w_q, w_kv, w_o):
    B, T, _ = x.shape
    M = mem.shape[1]
    q = (x @ w_q).reshape(B, T, XA_HEADS, XA_HEAD_DIM)
    k, v = jnp.split(mem @ w_kv, 2, axis=-1)
    k = k.reshape(B, M, XA_HEADS, XA_HEAD_DIM)
    v = v.reshape(B, M, XA_HEADS, XA_HEAD_DIM)
    s = jnp.einsum('bthd,bmhd->bhtm', q, k).astype(jnp.float32) * (XA_HEAD_DIM ** -0.5)
    p = jax.nn.softmax(s, axis=-1).astype(v.dtype)
    o = jnp.einsum('bhtm,bmhd->bthd', p, v).reshape(B, T, D_MODEL)
    return o @ w_o


def trunk(x, mem, ffn1_w_gu, ffn1_w_down, w_mix_in, conv_w, w_conv_out, gla_gate_w2, gla_gate_b,
          gla_norm_g, w_gla_out, w_mix_out, xa_w_q, xa_w_kv, xa_w_o, ffn2_w_gu, ffn2_w_down, ln_g, ln_b):
    for l in range(DEPTH):
        x = layer_norm(DN_ALPHA * x + 0.5 * swiglu(x, ffn1_w_gu[l], ffn1_w_down[l]), ln_g[l, 0], ln_b[l, 0])
        x = layer_norm(DN_ALPHA * x + parallel_mixer(x, w_mix_in[l], conv_w[l], w_conv_out[l], gla_gate_w2[l],
                                                     gla_gate_b[l], gla_norm_g[l], w_gla_out[l], w_mix_out[l]),
                       ln_g[l, 1], ln_b[l, 1])
        x = layer_norm(DN_ALPHA * x + memory_cross_attention(x, mem, xa_w_q[l], xa_w_kv[l], xa_w_o[l]), ln_g[l, 2], ln_b[l, 2])
        x = layer_norm(DN_ALPHA * x + 0.5 * swiglu(x, ffn2_w_gu[l], ffn2_w_down[l]), ln_g[l, 3], ln_b[l, 3])
    return x


def _normal(key, shape, scale):
    return jax.random.normal(key, shape, jnp.float32) * scale


def setup_inputs(seed: int = 0) -> dict:
    key = jax.random.key(seed)
    ks = jax.random.split(key, 24)
    L = DEPTH
    return {
        'x_prompt': _normal(ks[0], (BATCH, SEQ, D_MODEL), 1.0),
        'x_sample': _normal(ks[1], (DEC_BATCH, DEC_SEQ, D_MODEL), 1.0),
        'mem_prompt': _normal(ks[2], (BATCH, N_MEM, D_MODEL), 1.0),
        'mem_sample': _normal(ks[3], (DEC_BATCH, N_MEM, D_MODEL), 1.0),
        'ffn1_w_gu': _normal(ks[4], (L, D_MODEL, 2 * D_FF), D_MODEL ** -0.5),
        'ffn1_w_down': _normal(ks[5], (L, D_FF, D_MODEL), DN_BETA * D_FF ** -0.5),
        'w_mix_in': _normal(ks[6], (L, D_MODEL, D_IN), D_MODEL ** -0.5),
        'conv_w': _normal(ks[7], (L, CONV_W, D_CONV), CONV_W ** -0.5),
        'w_conv_out': _normal(ks[8], (L, D_CONV, D_MODEL), D_CONV ** -0.5),
        'gla_gate_w2': _normal(ks[9], (L, 2, GATE_RANK, D_GLA_K), GATE_RANK ** -0.5),
        'gla_gate_b': GATE_BIAS_MEAN + _normal(ks[10], (L, 2, D_GLA_K), 0.1),
        'gla_norm_g': 1.0 + _normal(ks[11], (L, HEAD_V), 0.01),
        'w_gla_out': _normal(ks[12], (L, D_GLA_V, D_MODEL), D_GLA_V ** -0.5),
        'w_mix_out': _normal(ks[13], (L, D_MODEL, D_MODEL), DN_BETA * D_MODEL ** -0.5),
        'xa_w_q': _normal(ks[14], (L, D_MODEL, D_MODEL), D_MODEL ** -0.5),
        'xa_w_kv': _normal(ks[15], (L, D_MODEL, 2 * D_MODEL), D_MODEL ** -0.5),
        'xa_w_o': _normal(ks[16], (L, D_MODEL, D_MODEL), DN_BETA * D_MODEL ** -0.5),
        'ffn2_w_gu': _normal(ks[17], (L, D_MODEL, 2 * D_FF), D_MODEL ** -0.5),
        'ffn2_w_down': _normal(ks[18], (L, D_FF, D_MODEL), DN_BETA * D_FF ** -0.5),
        'ln_g': 1.0 + _normal(ks[19], (L, 4, D_MODEL), 0.01),
        'ln_b': _normal(ks[20], (L, 4, D_MODEL), 0.01),
    }


def reference(x_prompt, x_sample, mem_prompt, mem_sample, ffn1_w_gu, ffn1_w_down, w_mix_in, conv_w, w_conv_out,
              gla_gate_w2, gla_gate_b, gla_norm_g, w_gla_out, w_mix_out, xa_w_q, xa_w_kv, xa_w_o,
              ffn2_w_gu, ffn2_w_down, ln_g, ln_b):
    weights = (ffn1_w_gu, ffn1_w_down, w_mix_in, conv_w, w_conv_out, gla_gate_w2, gla_gate_b, gla_norm_g,
               w_gla_out, w_mix_out, xa_w_q, xa_w_kv, xa_w_o, ffn2_w_gu, ffn2_w_down, ln_g, ln_b)
    y_prompt = trunk(x_prompt, mem_prompt, *weights)
    y_sample = trunk(x_sample, mem_sample, *weights)
    return (y_prompt, y_sample)
```

```python
import numpy as np
import concourse.bass as bass
import concourse.mybir as mybir
from concourse.bass_utils import run_bass_kernel_spmd

F32 = mybir.dt.float32
BF16 = mybir.dt.bfloat16
AF = mybir.ActivationFunctionType
ALU = mybir.AluOpType

D = 1024
DFF = 2816
DIN = 5152
NMEM = 256
ALPHA = float((2 * 4) ** 0.25)
SLAB = 256
NDMASEM = 8

C_H, C_GB, C_GC, C_Q, C_K, C_V, C_G, C_LRF, C_LRB, C_MC, C_MG = (
    0, 512, 1024, 1536, 1792, 2048, 2560, 3072, 3088, 3104, 4128)

K_IDENT = 0
K_ONESM = 128
K_ONESR = 256
K_TRID_F = 384
K_TRID_B = 512
K_MASK_F = 640
K_MASK_B = 768
K_TRIX_F = 896
K_TRIX_B = 900
K_EPS_LN = 904
K_EPS_RMS = 905
K_ONE = 906
K_ONES1 = 907
NCONST = 907 + 128


def make_consts():
    c = np.zeros((128, NCONST), np.float32)
    c[:, K_IDENT:K_IDENT + 128] = np.eye(128, dtype=np.float32)
    c[:, K_ONESM:K_ONESM + 128] = 1.0 / 1024.0
    c[:, K_ONESR:K_ONESR + 128] = 1.0 / 128.0
    j = np.arange(128)[:, None]
    i = np.arange(128)[None, :]
    same = (j // 64) == (i // 64)
    jj = j % 64
    ii = i % 64
    s = -1.0 / 16.0
    c[:, K_TRID_F:K_TRID_F + 128] = s * (same * ((jj <= ii).astype(np.float32) - (jj <= 31).astype(np.float32)))
    c[:, K_TRID_B:K_TRID_B + 128] = s * (same * ((jj >= ii).astype(np.float32) - (jj >= 32).astype(np.float32)))
    c[:, K_MASK_F:K_MASK_F + 128] = (same & (jj <= ii)).astype(np.float32)
    c[:, K_MASK_B:K_MASK_B + 128] = (same & (jj > ii)).astype(np.float32)
    jv = np.arange(128)
    for col in range(2):
        inch = (jv // 64) == col
        jl = jv % 64
        c[:, K_TRIX_F + col] = s * (inch & (jl <= 31))
        c[:, K_TRIX_F + 2 + col] = s * (inch & (jl >= 32))
        c[:, K_TRIX_B + col] = s * (inch & (jl >= 32))
        c[:, K_TRIX_B + 2 + col] = s * (inch & (jl <= 31))
    c[:, K_EPS_LN] = 1e-5
    c[:, K_EPS_RMS] = 1e-6
    c[:, K_ONE] = 1.0
    c[:, K_ONES1:K_ONES1 + 128] = 1.0
    return c


class Op:
    __slots__ = ("eng", "fn", "deps", "odeps", "sig", "semkey", "semval", "is_dma", "waits", "vc", "cost", "nbytes", "pri")


LAT_X = 0.45


class Prog:
    def __init__(self, nc):
        self.nc = nc
        self.ops = []
        self.last_w = {}
        self.readers = {}
        self.alias = {}
        self.shift = 0

    def reg(self, name, off, nbytes):
        self.alias[name] = tuple(("sb", i) for i in range(off // SLAB, (off + nbytes - 1) // SLAB + 1))

    def expand(self, keys):
        out = []
        for k in keys:
            a = self.alias.get(k)
            if a is None:
                out.append(k)
            else:
                out.extend(a)
        return out

    def op(self, eng, fn, r=(), w=(), dma=False, cost=0.3, nbytes=0):
        i = len(self.ops)
        o = Op()
        o.eng = eng
        o.fn = fn
        o.is_dma = dma
        o.sig = dma
        o.semkey = None
        o.semval = 0
        o.waits = None
        o.vc = None
        o.cost = cost
        o.nbytes = nbytes
        o.pri = i + self.shift
        rk = self.expand(r)
        wk = self.expand(w)
        deps = set()
        for k in rk:
            lw = self.last_w.get(k)
            if lw is not None:
                deps.add(lw)
        for k in wk:
            lw = self.last_w.get(k)
            if lw is not None:
                deps.add(lw)
            rd = self.readers.get(k)
            if rd:
                deps.update(rd)
        for k in rk:
            self.readers.setdefault(k, []).append(i)
        for k in wk:
            self.last_w[k] = i
            self.readers[k] = []
        if eng == "pe" and not dma:
            o.odeps = {d for d in deps if (self.ops[d].eng == "pe" and not self.ops[d].is_dma)}
            o.deps = deps - o.odeps
        else:
            o.odeps = set()
            o.deps = deps
        self.ops.append(o)
        return i

    def schedule(self):
        import heapq
        ops = self.ops
        n = len(ops)
        succ = [[] for _ in range(n)]
        indeg = [0] * n
        for i, o in enumerate(ops):
            for d in o.deps:
                succ[d].append(i)
            for d in o.odeps:
                succ[d].append(i)
            indeg[i] = len(o.deps) + len(o.odeps)
        engs = ["pe", "act", "dve", "pool", "sp"]
        free = {e: 0.0 for e in engs}
        avail = {e: [] for e in engs}
        pend = {e: [] for e in engs}
        rtime = [0.0] * n
        fin = [0.0] * n
        for i, o in enumerate(ops):
            if indeg[i] == 0:
                heapq.heappush(pend[o.eng], (0.0, o.pri, i))
        dma_free = 0.0
        order = []
        while len(order) < n:
            best = None
            for e in engs:
                ft = free[e]
                pe_ = pend[e]
                av = avail[e]
                while pe_ and pe_[0][0] <= ft:
                    _, pr_, i = heapq.heappop(pe_)
                    heapq.heappush(av, (pr_, i))
                if av:
                    cand = (ft, av[0][0], av[0][1], e, True)
                elif pe_:
                    cand = (pe_[0][0], pe_[0][1], pe_[0][2], e, False)
                else:
                    continue
                if best is None or (cand[0], cand[1], cand[2]) < (best[0], best[1], best[2]):
                    best = cand
            st, _pr, i, e, fa = best
            if fa:
                heapq.heappop(avail[e])
            else:
                heapq.heappop(pend[e])
            o = ops[i]
            if o.is_dma:
                issue = 0.25 if e == "sp" else 0.6
                free[e] = st + issue
                t0 = max(st + issue, dma_free)
                xfer = o.nbytes / 160000.0
                dma_free = t0 + xfer
                fin[i] = t0 + xfer + 2.0
            else:
                free[e] = st + o.cost
                fin[i] = st + o.cost
            order.append(i)
            for s_ in succ[i]:
                so = ops[s_]
                lat = 0.0 if (so.eng == "pe" and e == "pe" and not o.is_dma and not so.is_dma) else LAT_X
                t = fin[i] + lat
                if t > rtime[s_]:
                    rtime[s_] = t
                indeg[s_] -= 1
                if indeg[s_] == 0:
                    heapq.heappush(pend[so.eng], (rtime[s_], so.pri, s_))
        self.order = order
        self.makespan = max(fin)
        self.fin = fin
        self.busy = {e: 0.0 for e in engs}
        for o in ops:
            if not o.is_dma:
                self.busy[o.eng] += o.cost

    def finalize(self, final_wait_ops):
        ops = self.ops
        self.schedule()
        order = self.order
        self.eng_ops = {"pe": [], "act": [], "dve": [], "pool": [], "sp": []}
        for i in order:
            self.eng_ops[ops[i].eng].append(i)
        dcount = {}
        slot_last = {}
        for i in order:
            o = ops[i]
            if o.is_dma:
                c = dcount.get(o.eng, 0)
                dcount[o.eng] = c + 1
                o.semkey = (o.eng, c % NDMASEM)
                prev = slot_last.get(o.semkey)
                if prev is not None:
                    o.deps = set(o.deps)
                    o.deps.add(prev)
                slot_last[o.semkey] = i
        fin = Op()
        fin.eng = "sp"; fin.fn = None; fin.is_dma = False; fin.sig = False
        fin.semkey = None; fin.semval = 0; fin.waits = None; fin.vc = None
        fin.deps = set(final_wait_ops); fin.odeps = set(); fin.cost = 0; fin.nbytes = 0; fin.pri = 1 << 60
        ops.append(fin)
        fi = len(ops) - 1
        order = order + [fi]
        self.eng_ops["sp"].append(fi)
        for o in ops:
            for d in o.deps:
                ops[d].sig = True
        cnt = {}
        slotcnt = {}
        for i in order:
            o = ops[i]
            if not o.sig:
                continue
            if o.is_dma:
                slotcnt[o.semkey] = slotcnt.get(o.semkey, 0) + 16
                o.semval = slotcnt[o.semkey]
            else:
                o.semkey = o.eng
                cnt[o.eng] = cnt.get(o.eng, 0) + 1
                o.semval = cnt[o.eng]
        clock = {e: {} for e in self.eng_ops}
        nwaits = 0
        for i in order:
            o = ops[i]
            ck = clock[o.eng]
            waits = {}
            for d in o.deps:
                dd = ops[d]
                if ck.get(dd.semkey, 0) >= dd.semval:
                    continue
                if waits.get(dd.semkey, 0) < dd.semval:
                    waits[dd.semkey] = dd.semval
            for d in o.deps:
                dd = ops[d]
                for k, v in dd.vc.items():
                    if ck.get(k, 0) < v:
                        ck[k] = v
            o.waits = list(waits.items())
            nwaits += len(o.waits)
            if o.sig:
                vc = dict(ck)
                vc[o.semkey] = o.semval
                o.vc = vc
        self.nwaits = nwaits

    def emit(self, block, sems):
        ops = self.ops

        def run(eng_name, e):
            for i in self.eng_ops[eng_name]:
                o = ops[i]
                for (k, v) in o.waits:
                    e.wait_ge(sems[k], v)
                if o.fn is None:
                    continue
                ins = o.fn(e)
                if o.sig:
                    ins.then_inc(sems[o.semkey], 16 if o.is_dma else 1)

        @block.sync
        def _(e):
            run("sp", e)

        @block.scalar
        def _(e):
            run("act", e)

        @block.vector
        def _(e):
            run("dve", e)

        @block.gpsimd
        def _(e):
            run("pool", e)

        @block.tensor
        def _(e):
            run("pe", e)


class Arena:
    def __init__(self, P, nc, base, limit, tag):
        self.P = P; self.nc = nc; self.off = base; self.limit = limit; self.tag = tag; self.n = 0

    def alloc(self, name, shape, dtype, split=None, reuse=None):
        esz = 4 if dtype == F32 else 2
        per = 1
        for s in shape[1:]:
            per *= s
        nbytes = per * esz
        self.off = (self.off + SLAB - 1) // SLAB * SLAB
        if reuse is not None:
            assert reuse[1] == self.off, (name, reuse[1], self.off)
            h = reuse[0]
        else:
            h = self.nc.alloc_sbuf_tensor_at(f"{self.tag}_{name}", list(shape), dtype, offset=self.off)
        self.last_off = self.off
        self.P.reg(name, self.off, nbytes)
        if split:
            sub = nbytes // split
            for i in range(split):
                self.P.reg(f"{name}.{i}", self.off + i * sub, sub)
        self.off += nbytes
        assert self.off <= self.limit, (self.tag, name, self.off, self.limit)
        return h


def build(L=4, NSEG=3, SEGLEN=2048, TTF=384):
    NT = NSEG * SEGLEN
    TT = 512
    nc = bass.Bass("TRN2", target_bir_lowering=False)
    P = Prog(nc)

    def din(name, shape, dt=F32):
        return nc.dram_tensor(name, list(shape), dt, kind="ExternalInput").ap()

    x_in = din("x_in", [NT, D])
    mem_in = din("mem_in", [NSEG * NMEM, D])
    w_gu1 = din("ffn1_w_gu", [L, D, 2 * DFF]); w_d1 = din("ffn1_w_down", [L, DFF, D])
    w_gu2 = din("ffn2_w_gu", [L, D, 2 * DFF]); w_d2 = din("ffn2_w_down", [L, DFF, D])
    w_in = din("w_mix_in", [L, D, DIN])
    w_co = din("w_conv_out", [L, 512, D]); w_go = din("w_gla_out", [L, 512, D])
    w_mo = din("w_mix_out", [L, D, D])
    w_g2 = din("gla_gate_w2", [L, 2, 16, 256])
    w_xq = din("xa_w_q", [L, D, D]); w_xkv = din("xa_w_kv", [L, D, 2 * D]); w_xo = din("xa_w_o", [L, D, D])
    consts_d = din("consts", [128, NCONST])
    flags_d = din("flags", [128, 2])
    lnp_d = din("lnp", [128, L * 64])
    convw_d = din("convw", [128, L * 12])
    gng_d = din("gng", [128, L])
    gbias_d = din("gbias", [128, L * 512])
    y_out = nc.dram_tensor("y_out", [NT, D], F32, kind="ExternalOutput").ap()

    X = [nc.dram_tensor(f"Xs{i}", [D, NT], F32, kind="Internal").ap() for i in range(2)]
    Xv = [x.rearrange("(k p) t -> p k t", p=128) for x in X]
    OB = nc.dram_tensor("OBs", [512, NT], F32, kind="Internal").ap().rearrange("(k p) t -> p k t", p=128)
    ONs = nc.dram_tensor("ONs", [512, NT], BF16, kind="Internal").ap().rearrange("(k p) t -> p k t", p=128)
    CGs = nc.dram_tensor("CGs", [512, NT], BF16, kind="Internal").ap().rearrange("(k p) t -> p k t", p=128)
    QSs = nc.dram_tensor("QSs", [256, NT], BF16, kind="Internal").ap().rearrange("(k p) t -> p k t", p=128)
    KSs = nc.dram_tensor("KSs", [256, NT], BF16, kind="Internal").ap().rearrange("(k p) t -> p k t", p=128)
    VSs = nc.dram_tensor("VSs", [NT, 512], BF16, kind="Internal").ap()
    MEMT = nc.dram_tensor("MEMTs", [D, NSEG * NMEM], F32, kind="Internal").ap().rearrange("(k p) t -> p k t", p=128)

    SB0 = (nc.SBUF_PARTITION_SIZE_BYTES - nc.sbuf_bytes_remaining + 255) // 256 * 256
    LIMIT = nc.SBUF_PARTITION_SIZE_BYTES - 256
    AP_ = Arena(P, nc, SB0, LIMIT, "g")
    cst = AP_.alloc("cst", [128, NCONST], F32)
    cstb = AP_.alloc("cstb", [128, NCONST], BF16)
    flags = AP_.alloc("flags", [128, 2], F32)
    lnp = AP_.alloc("lnp", [128, L * 64], F32)
    convw = AP_.alloc("convw", [128, L * 12], F32)
    gng = AP_.alloc("gng", [128, L], F32)
    Sst = AP_.alloc("Sst", [128, 2, 128], F32, split=2)
    PBASE = (AP_.off + 1023) // 1024 * 1024

    ps = [nc.alloc_psum_tensor(f"psb{i}", [128, 512], F32) for i in range(8)]

    ident = cst[:, K_IDENT:K_IDENT + 128]
    ones_m = cst[:, K_ONESM:K_ONESM + 128]
    ones_r = cst[:, K_ONESR:K_ONESR + 128]
    identb = cstb[:, K_IDENT:K_IDENT + 128]
    onesb = cstb[:, K_ONES1:K_ONES1 + 128]
    onesm_b = cstb[:, K_ONESM:K_ONESM + 128]
    onesr_b = cstb[:, K_ONESR:K_ONESR + 128]

    def xkeys(w, t0, n):
        return [("X", w, b) for b in range(t0 // 128, (t0 + n + 127) // 128)]

    def fsz(ap):
        n_ = 1
        for v in ap.shape[1:]:
            n_ *= v
        return n_

    def mm(out, lhsT, rhs, start, stop, r, w):
        nn = fsz(rhs)
        c = max(nn / 2300.0, 0.095)
        if lhsT.dtype == F32:
            c = c * 4.5 + 0.2
        return P.op("pe", lambda e: e.matmul(out, lhsT, rhs, start=start, stop=stop), r=r, w=w, cost=c)

    def tr(out, in_, idn, r, w):
        c = 0.1 if in_.dtype == BF16 else 0.22
        return P.op("pe", lambda e: e.transpose(out, in_, idn), r=r, w=w, cost=c)

    def act(out, in_, func, r, w, bias=None, scale=None):
        kw = {}
        if bias is not None:
            kw["bias"] = bias
        if scale is not None:
            kw["scale"] = scale
        c = 0.22 + fsz(out) / 1400.0
        return P.op("act", lambda e: e.activation(out, in_, func, **kw), r=r, w=w, cost=c)

    def vcost(eng, out):
        if eng == "pool":
            return 0.3 + fsz(out) / 480.0
        return 0.1 + fsz(out) / 960.0

    def tt(eng, out, in0, in1, op, r, w):
        return P.op(eng, lambda e: e.tensor_tensor(out, in0, in1, op), r=r, w=w, cost=vcost(eng, out))

    def ts(eng, out, in0, s1, s2, op0, op1, r, w):
        if s2 is None:
            return P.op(eng, lambda e: e.tensor_scalar(out, in0, s1, None, op0), r=r, w=w, cost=vcost(eng, out))
        return P.op(eng, lambda e: e.tensor_scalar(out, in0, s1, s2, op0, op1), r=r, w=w, cost=vcost(eng, out))

    def stt(out, in0, sc, in1, op0, op1, r, w):
        return P.op("dve", lambda e: e.scalar_tensor_tensor(out, in0, sc, in1, op0, op1), r=r, w=w,
                    cost=vcost("dve", out))

    def cp(eng, out, in_, r, w):
        if eng == "act":
            return P.op("act", lambda e: e.copy(out, in_), r=r, w=w, cost=0.22 + fsz(out) / 1400.0)
        return P.op(eng, lambda e: e.tensor_copy(out, in_), r=r, w=w, cost=vcost(eng, out))

    def amul(out, in_, val, r, w):
        return P.op("act", lambda e: e.mul(out, in_, val), r=r, w=w, cost=0.22 + fsz(out) / 1400.0)

    def dma(q, out, in_, r, w, **kw):
        nb = 128 * fsz(out) * (4 if in_.dtype == F32 else 2)
        if out.shape[0] < 128:
            nb = out.shape[0] * fsz(out) * 4
        return P.op(q, lambda e: e.dma_start(out=out, in_=in_, **kw), r=r, w=w, dma=True, nbytes=nb)

    def mset(eng, ap, val, w):
        return P.op(eng, lambda e: e.memset(ap, val), r=[], w=w, cost=vcost(eng, ap))

    def recip(out, in_, r, w):
        return P.op("dve", lambda e: e.reciprocal(out, in_), r=r, w=w, cost=vcost("dve", out))

    WP = 512

    def WK(name, k, c0):
        return f"{name}.{k}.{c0 // WP}"

    def load_w(A, W, wv, name, ncols, order=None):
        base = A.last_off
        npc = (ncols + WP - 1) // WP
        for k in range(8):
            for p_ in range(npc):
                c0 = p_ * WP
                c1 = min(ncols, c0 + WP)
                P.reg(f"{name}.{k}.{p_}", base + (k * ncols + c0) * 2, (c1 - c0) * 2)
        for p_ in (order if order is not None else range(npc)):
            c0 = p_ * WP
            c1 = min(ncols, c0 + WP)
            for k in range(8):
                dma("pool", W[:, k, c0:c1], wv[:, k, c0:c1], [], [f"{name}.{k}.{p_}"])

    cpi = [0]

    def cpany(out, in_, r, w):
        cpi[0] += 1
        return cp("act" if cpi[0] % 2 else "dve", out, in_, r, w)

    def sigmoid_chain(buf, src, rk, bk):
        act(buf, src, AF.Exp, rk, [bk], scale=-1.0)
        act(buf, buf, AF.Ln, [bk, "cst"], [bk], bias=cst[:, K_ONE:K_ONE + 1])
        act(buf, buf, AF.Exp, [bk], [bk], scale=-1.0)

    dma("sp", cst[:], consts_d, [], ["cst"])
    dma("sp", flags[:], flags_d, [], ["flags"])
    dma("sp", lnp[:], lnp_d, [], ["lnp"])
    dma("sp", convw[:], convw_d, [], ["convw"])
    dma("sp", gng[:], gng_d, [], ["gng"])
    cp("dve", cstb[:], cst[:], ["cst"], ["cstb"])

    def transpose_in(src, n_tok, dstv, dkey):
        A = Arena(P, nc, PBASE + 92160, LIMIT, "tin")
        xin = [A.alloc(f"xin{i}", [128, 2, D], F32) for i in range(2)]
        stg = [A.alloc(f"stg{i}", [128, 8, 256], F32) for i in range(2)]
        for ti in range(n_tok // 256):
            t0 = ti * 256
            sl = ti % 2
            dma("sp", xin[sl][:], src[t0:t0 + 256, :].rearrange("(s p) f -> p s f", p=128), [], [f"xin{sl}"])
            for k in range(8):
                b = k % 4
                for s in range(2):
                    tr(ps[b][:, s * 128:(s + 1) * 128], xin[sl][:, s, k * 128:(k + 1) * 128], ident,
                       [f"xin{sl}", "cst"], [f"ps{b}"])
                cpany(stg[sl][:, k, :], ps[b][:, 0:256], [f"ps{b}"], [f"stg{sl}"])
            dma("sp", dstv[:, :, t0:t0 + 256], stg[sl][:], [f"stg{sl}"], dkey(t0, 256))

    transpose_in(x_in, NT, Xv[0], lambda t0, n: xkeys(0, t0, n))
    transpose_in(mem_in, NSEG * NMEM, MEMT, lambda t0, n: [("MEMT", b) for b in range(t0 // 128, (t0 + n) // 128)])

    def ln_epilogue(xf, xfk, n, l, j, vb, sqb, lt, rstd, psm, psv, dstv, t0, dkeys):
        gcol = ((l * 4 + j) * 2 + 0) * 8
        bcol = ((l * 4 + j) * 2 + 1) * 8
        for d in range(8):
            cp("act", vb[:, d, :], xf[:, d, :], [f"{xfk}.{d}"], [f"vb.{d}"])
            mm(ps[psm][:, 0:n], onesm_b, vb[:, d, :], d == 0, d == 7, [f"vb.{d}", "cstb"], [f"ps{psm}"])
        tt("dve", xf[:], xf[:], ps[psm][:, 0:n].unsqueeze(1).to_broadcast([128, 8, n]), ALU.subtract,
           [xfk, f"ps{psm}"], [xfk])
        for d in range(8):
            act(sqb[d % 2][:], xf[:, d, :], AF.Square, [f"{xfk}.{d}"], [f"sqb{d % 2}"])
            mm(ps[psv][:, 0:n], onesm_b, sqb[d % 2][:], d == 0, d == 7, [f"sqb{d % 2}", "cstb"], [f"ps{psv}"])
        act(lt[:], ps[psv][:, 0:n], AF.Ln, [f"ps{psv}", "cst"], ["lt"], bias=cst[:, K_EPS_LN:K_EPS_LN + 1])
        act(rstd[:], lt[:], AF.Exp, ["lt"], ["rstd"], scale=-0.5)
        tt("dve", xf[:], xf[:], rstd[:].unsqueeze(1).to_broadcast([128, 8, n]), ALU.mult, [xfk, "rstd"], [xfk])
        for d in range(8):
            if d % 2 == 0:
                ts("pool", xf[:, d, :], xf[:, d, :], lnp[:, gcol + d:gcol + d + 1], lnp[:, bcol + d:bcol + d + 1],
                   ALU.mult, ALU.add, [f"{xfk}.{d}", "lnp"], [f"{xfk}.{d}"])
            else:
                act(xf[:, d, :], xf[:, d, :], AF.Identity, [f"{xfk}.{d}", "lnp"], [f"{xfk}.{d}"],
                    bias=lnp[:, bcol + d:bcol + d + 1], scale=lnp[:, gcol + d:gcol + d + 1])
        return dma("sp", dstv[:, :, t0:t0 + n], xf[:], [xfk], dkeys)

    def ffn_phase(l, j, wgu_d, wd_d, cur):
        A = Arena(P, nc, PBASE, LIMIT, f"f{l}{j}")
        Wgu = A.alloc("Wgu", [128, 8, 2 * DFF], BF16)
        load_w(A, Wgu, wgu_d[l].rearrange("(k p) f -> p k f", p=128), "Wgu", 2 * DFF, order=[0, 5, 6, 1, 7, 2, 8, 3, 9, 4, 10])
        Wd = A.alloc("Wd", [128, 22, D], BF16, split=22)
        n = TTF
        xb = [A.alloc(f"xb{i}", [128, 8, n], BF16) for i in range(2)]
        xf = A.alloc("xf", [128, 8, n], F32, split=8)
        h = A.alloc("h", [128, 22, n], BF16, split=22)
        vb = A.alloc("vb", [128, 8, n], BF16, split=8)
        sg = [A.alloc(f"sg{i}", [128, n], F32) for i in range(2)]
        gs = [A.alloc(f"gs{i}", [128, n], F32) for i in range(2)]
        sqb = [A.alloc(f"sqb{i}", [128, n], BF16) for i in range(2)]
        lt = A.alloc("lt", [128, n], F32)
        rstd = A.alloc("rstd", [128, n], F32)
        wdv = wd_d[l].rearrange("(f p) d -> p f d", p=128)
        for f in range(22):
            dma("pool", Wd[:, f, :], wdv[:, f, :], [], [f"Wd.{f}"], max_dma_last_dim=8192)
        ntile = NT // n

        def load(ti):
            t0 = ti * n
            dma("pool", xb[ti % 2][:], Xv[cur][:, :, t0:t0 + n], xkeys(cur, t0, n), [f"xb{ti % 2}"])

        load(0)
        for ti in range(ntile):
            t0 = ti * n
            sl = ti % 2
            if ti + 1 < ntile:
                load(ti + 1)
            dma("sp", xf[:], Xv[cur][:, :, t0:t0 + n], xkeys(cur, t0, n), ["xf"])
            for f in range(22):
                bg = (f % 2) * 2
                bu = bg + 1
                for k in range(8):
                    mm(ps[bg][:, 0:n], Wgu[:, k, f * 128:(f + 1) * 128], xb[sl][:, k, :], k == 0, k == 7,
                       [WK("Wgu", k, f * 128), f"xb{sl}"], [f"ps{bg}"])
                for k in range(8):
                    mm(ps[bu][:, 0:n], Wgu[:, k, DFF + f * 128:DFF + (f + 1) * 128], xb[sl][:, k, :], k == 0, k == 7,
                       [WK("Wgu", k, DFF + f * 128), f"xb{sl}"], [f"ps{bu}"])
                sigmoid_chain(sg[f % 2][:], ps[bg][:, 0:n], [f"ps{bg}"], f"sg{f % 2}")
                stt(gs[f % 2][:], sg[f % 2][:], 0.5, ps[bg][:, 0:n], ALU.mult, ALU.mult, [f"sg{f % 2}", f"ps{bg}"], [f"gs{f % 2}"])
                tt("dve", h[:, f, :], gs[f % 2][:], ps[bu][:, 0:n], ALU.mult, [f"gs{f % 2}", f"ps{bu}"], [f"h.{f}"])
            for d in range(8):
                b = 4 + d % 2
                for f in range(22):
                    mm(ps[b][:, 0:n], Wd[:, f, d * 128:(d + 1) * 128], h[:, f, :], f == 0, f == 21,
                       [f"Wd.{f}", f"h.{f}"], [f"ps{b}"])
                stt(xf[:, d, :], xf[:, d, :], ALPHA, ps[b][:, 0:n], ALU.mult, ALU.add, [f"ps{b}", f"xf.{d}"], [f"xf.{d}"])
            ln_epilogue(xf, "xf", n, l, j, vb, sqb, lt, rstd, 6, 7, Xv[1 - cur], t0, xkeys(1 - cur, t0, n))

    def gla_tile(bufs, kn, l, direction, first_in_scan, seg_start, link_flag, Win, xbh, XB, qk):
        (qz, kin, vz, lrs, zs, zh, zl, eq, ek, sml, kintok, attm, kvs, sps, w2b, gbias) = bufs
        fwd = direction == 0
        c_lr = C_LRF if fwd else C_LRB
        k_trid = K_TRID_F if fwd else K_TRID_B
        k_trix = K_TRIX_F if fwd else K_TRIX_B
        k_mask = K_MASK_F if fwd else K_MASK_B
        P.shift = -CHAIN_SHIFT if fwd else 0
        for k in range(8):
            mm(ps[4][0:16, :], Win[:, k, c_lr:c_lr + 16], xbh[:, k, 0:512], k == 0, k == 7, [WK("Win", k, c_lr), XB], ["ps4"])
        cp("act", lrs[0:16, :], ps[4][0:16, :], ["ps4"], [kn["lrs"]])
        mode, qraw, kraw, QR, KR = qk
        P.shift = 0
        for s in range(4):
            if mode == "load":
                break
            b = 7 if s % 2 == 0 else 4
            for k in range(8):
                mm(ps[b][:, :], xbh[:, k, s * 128:(s + 1) * 128], Win[:, k, C_V:C_V + 512], k == 0, k == 7,
                   [WK("Win", k, C_V), XB], [f"ps{b}"])
            cp("act", vz[0:64, s, 0, :], ps[b][0:64, :], [f"ps{b}"], [kn["vz"]])
            cp("dve", vz[64:128, s, 1, :], ps[b][64:128, :], [f"ps{b}"], [kn["vz"]])
        P.shift = -CHAIN_SHIFT if fwd else 0
        for s in range(4):
            b = 5 + s // 2
            mm(ps[b][:, (s % 2) * 256:(s % 2) * 256 + 256], lrs[0:16, s * 128:(s + 1) * 128],
               w2b[0:16, direction, :], True, True, [kn["lrs"], "w2b"], [f"ps{b}"])
        for hf in range(2):
            tt("dve", zs[:, 2 * hf:2 * hf + 2, :], ps[5 + hf][:, :].rearrange("p (s d) -> p s d", s=2),
               gbias[:, direction, :].unsqueeze(1).to_broadcast([128, 2, 256]), ALU.add,
               [f"ps{5 + hf}", "gbias"], [kn["zs"]])
        act(zs[:], zs[:], AF.Exp, [kn["zs"]], [kn["zs"]], scale=-1.0)
        act(zs[:], zs[:], AF.Ln, [kn["zs"], "cst"], [kn["zs"]], bias=cst[:, K_ONE:K_ONE + 1])
        cp("act", zh[:], zs[:], [kn["zs"]], [kn["zh"]])
        tt("dve", zl[:], zs[:], zh[:], ALU.subtract, [kn["zs"], kn["zh"]], [kn["zl"]])
        for dch in range(2):
            for s in range(4):
                mm(ps[5 + dch][:, s * 128:(s + 1) * 128], zh[:, s, dch * 128:(dch + 1) * 128],
                   cstb[:, k_trid:k_trid + 128], True, False, [kn["zh"], "cstb"], [f"ps{5 + dch}"])
                mm(ps[5 + dch][:, s * 128:(s + 1) * 128], zl[:, s, dch * 128:(dch + 1) * 128],
                   cstb[:, k_trid:k_trid + 128], False, True, [kn["zl"], "cstb"], [f"ps{5 + dch}"])
        for dch in range(2):
            for s in range(4):
                o4 = (dch * 4 + s) * 4
                mm(ps[4][:, o4:o4 + 4], zh[:, s, dch * 128:(dch + 1) * 128], cstb[:, k_trix:k_trix + 4],
                   True, False, [kn["zh"], "cstb"], ["ps4"])
                mm(ps[4][:, o4:o4 + 4], zl[:, s, dch * 128:(dch + 1) * 128], cstb[:, k_trix:k_trix + 4],
                   False, True, [kn["zl"], "cstb"], ["ps4"])
        for dch in range(2):
            act(eq[:, dch, :], ps[5 + dch][:, :], AF.Exp, [f"ps{5 + dch}"], [kn["eq"]])
            act(ek[:, dch, :], ps[5 + dch][:, :], AF.Exp, [f"ps{5 + dch}"], [kn["ek"]], scale=-1.0)
        xv4 = ps[4][:, 0:32].rearrange("p (a c) -> p a c", c=4)
        act(sml[:, 0, :].rearrange("p (a c) -> p a c", c=2), xv4[:, :, 0:2], AF.Exp, ["ps4"], [kn["sml"]])
        act(sml[:, 1, :].rearrange("p (a c) -> p a c", c=2), xv4[:, :, 2:4], AF.Exp, ["ps4"], [kn["sml"]])
        tt("dve", sml[:, 2, :], sml[:, 0, :], sml[:, 1, :], ALU.mult, [kn["sml"]], [kn["sml"]])
        for dch in range(2):
            bq = 5 + dch
            if mode == "save":
                for k in range(8):
                    mm(ps[bq][:, :], Win[:, k, C_Q + dch * 128:C_Q + (dch + 1) * 128], xbh[:, k, 0:512], k == 0, k == 7,
                       [WK("Win", k, C_Q + dch * 128), XB], [f"ps{bq}"])
                cp("act", qraw[:, dch, :], ps[bq][:, :], [f"ps{bq}"], [QR])
            for hh in range(2):
                pr = slice(hh * 64, hh * 64 + 64)
                stt(qz[pr, 2 * dch + hh, :], qraw[pr, dch, :], 0.125, eq[pr, dch, :], ALU.mult, ALU.mult,
                    [QR, kn["eq"]], [kn["qz"]])
        for dch in range(2):
            bk = 7 if dch == 0 else 4
            if mode == "save":
                for k in range(8):
                    mm(ps[bk][:, :], Win[:, k, C_K + dch * 128:C_K + (dch + 1) * 128], xbh[:, k, 0:512], k == 0, k == 7,
                       [WK("Win", k, C_K + dch * 128), XB], [f"ps{bk}"])
                cp("act", kraw[:, dch, :], ps[bk][:, :], [f"ps{bk}"], [KR])
            tt("dve", kin[:, dch, :], kraw[:, dch, :], ek[:, dch, :], ALU.mult, [KR, kn["ek"]], [kn["kin"]])
        P.shift = 0
        pst = ps[5][:, :].bitcast(BF16)
        for s in range(4):
            for dch in range(2):
                o = (s * 2 + dch) * 128
                tr(pst[:, o:o + 128], kin[:, dch, s * 128:(s + 1) * 128], identb, [kn["kin"], "cstb"], ["ps5"])
        cp("act", kintok[:].rearrange("p s d -> p (s d)"), pst[:, :], ["ps5"], [kn["kintok"]])
        for s in range(4):
            b = 6 + s % 2
            for hd in range(4):
                mm(ps[b][:, hd * 128:(hd + 1) * 128], kin[:, hd // 2, s * 128:(s + 1) * 128],
                   qz[:, hd, s * 128:(s + 1) * 128], True, True, [kn["kin"], kn["qz"]], [f"ps{b}"])
            tt("dve", attm[:, s, :, :], ps[b][:, :].rearrange("p (h i) -> p h i", h=4),
               cst[:, k_mask:k_mask + 128].unsqueeze(1).to_broadcast([128, 4, 128]), ALU.mult,
               [f"ps{b}", "cst"], [kn["attm"]])
        for c in range(8):
            s = c // 2
            par = c % 2
            b = 4 if c % 2 == 0 else 5
            for pair in range(2):
                for hh in range(2):
                    hd = 2 * pair + hh
                    mm(ps[b][hh * 64:hh * 64 + 64, pair * 128:(pair + 1) * 128],
                       kintok[:, s, hd * 64:(hd + 1) * 64], vz[:, s, par, hd * 128:(hd + 1) * 128],
                       True, True, [kn["kintok"], kn["vz"]], [f"ps{b}"])
            for pair in range(2):
                col = pair * 8 + c
                ts("dve", kvs[:, c, pair, :], ps[b][:, pair * 128:(pair + 1) * 128], sml[:, 1, col:col + 1], None,
                   ALU.mult, None, [f"ps{b}", kn["sml"]], [kn["kvs"] + f".{c}"])
        if seg_start:
            if first_in_scan or link_flag is None:
                mset("dve", Sst[:], 0.0, ["Sst"])
            else:
                ts("dve", Sst[:], Sst[:], flags[:, link_flag:link_flag + 1], None, ALU.mult, None,
                   ["Sst", "flags"], ["Sst"])
        order = range(8) if fwd else range(7, -1, -1)
        for c in order:
            for pair in range(2):
                col = pair * 8 + c
                act(sps[:, c, pair, :], Sst[:, pair, :], AF.Copy, [f"Sst.{pair}", kn["sml"]], [kn["sps"] + f".{c}"],
                    scale=sml[:, 0, col:col + 1])
                stt(Sst[:, pair, :], Sst[:, pair, :], sml[:, 2, col:col + 1], kvs[:, c, pair, :], ALU.mult, ALU.add,
                    [f"Sst.{pair}", kn["sml"], kn["kvs"] + f".{c}"], [f"Sst.{pair}"])
        obanks = [0, 1, 2, 3]
        for hd in range(4):
            b = obanks[hd]
            for s in range(4):
                mm(ps[b][:, s * 128:(s + 1) * 128], vz[:, s, 0, hd * 128:(hd + 1) * 128],
                   attm[:, s, hd, :], True, False, [kn["vz"], kn["attm"]], [f"ps{b}"])
                mm(ps[b][:, s * 128:(s + 1) * 128], vz[:, s, 1, hd * 128:(hd + 1) * 128],
                   attm[:, s, hd, :], False, False, [kn["vz"], kn["attm"]], [f"ps{b}"])
                for par in range(2):
                    c = 2 * s + par
                    mm(ps[b][:, c * 64:(c + 1) * 64], sps[:, c, hd // 2, :], qz[:, hd, c * 64:(c + 1) * 64],
                       False, par == 1, [kn["sps"] + f".{c}", kn["qz"]], [f"ps{b}"])
        return obanks

    GLA_SPECS = [("qz", [128, 4, 512], BF16, None), ("kin", [128, 2, 512], BF16, None),
                 ("vz", [128, 4, 2, 512], BF16, None), ("lrs", [128, 512], BF16, None),
                 ("zs", [128, 4, 256], F32, None), ("zh", [128, 4, 256], BF16, None), ("zl", [128, 4, 256], BF16, None),
                 ("eq", [128, 2, 512], F32, None), ("ek", [128, 2, 512], F32, None), ("sml", [128, 3, 16], F32, None),
                 ("kintok", [128, 4, 256], BF16, None), ("attm", [128, 4, 4, 128], BF16, None),
                 ("kvs", [128, 8, 2, 128], F32, 8), ("sps", [128, 8, 2, 128], BF16, 8)]

    shared = {}

    def gla_alloc(A, dbl, reuse=False):
        w2b = A.alloc("w2b", [128, 2, 256], BF16, reuse=shared["w2b"] if reuse else None)
        if not reuse:
            shared["w2b"] = (w2b, A.last_off)
        gbias = A.alloc("gbias", [128, 2, 256], F32, reuse=shared["gbias"] if reuse else None)
        if not reuse:
            shared["gbias"] = (gbias, A.last_off)
        slots = []
        kns = []
        single = {}
        for sl in range(2):
            bl = []
            kn = {}
            for (nm, shp, dt, sp_) in GLA_SPECS:
                if nm in dbl:
                    key = f"{nm}{sl}"
                    bl.append(A.alloc(key, shp, dt, split=sp_))
                else:
                    key = nm
                    if nm not in single:
                        single[nm] = A.alloc(key, shp, dt, split=sp_)
                    bl.append(single[nm])
                kn[nm] = key
            slots.append(tuple(bl) + (w2b, gbias))
            kns.append(kn)
        return slots, kns

    def gla_init(slots, kns, l, load=True):
        done = set()
        for sl in range(2):
            for nm, idx in (("qz", 0), ("vz", 2)):
                key = kns[sl][nm]
                if key in done:
                    continue
                done.add(key)
                mset("pool", slots[sl][idx][:], 0.0, [key])
        if not load:
            return
        w2b, gbias = slots[0][-2], slots[0][-1]
        dma("pool", w2b[0:16, :, :], w_g2[l].rearrange("d r c -> r d c"), [], ["w2b"])
        dma("sp", gbias[:], gbias_d[:, l * 512:(l + 1) * 512].rearrange("p (d c) -> p d c", d=2), [], ["gbias"])

    NTILE = NT // TT
    TPS = SEGLEN // TT
    CHAIN_SHIFT = 700

    def load_xbh(xbh, XB, cur, t0, halo):
        dma("pool", xbh[:, :, 0:512], Xv[cur][:, :, t0:t0 + 512], xkeys(cur, t0, 512), [XB])
        if halo:
            tl = max(t0 - 1, 0)
            trr = min(t0 + 512, NT - 1)
            dma("pool", xbh[:, :, 512:513], Xv[cur][:, :, tl:tl + 1], xkeys(cur, tl, 1), [XB], allow_slow_non_contiguous=True)
            dma("pool", xbh[:, :, 513:514], Xv[cur][:, :, trr:trr + 1], xkeys(cur, trr, 1), [XB], allow_slow_non_contiguous=True)

    def mixer_pass_b(l, cur):
        A = Arena(P, nc, PBASE, LIMIT, f"mb{l}")
        Win = A.alloc("Win", [128, 8, 3104], BF16)
        shared["Win"] = (Win, A.last_off)
        load_w(A, Win, w_in[l].rearrange("(k p) f -> p k f", p=128), "Win", 3104, order=[6, 4, 3, 5, 0, 1, 2])
        xbhs = [A.alloc(f"xbh{i}", [128, 8, 514], BF16) for i in range(2)]
        slots, kns = gla_alloc(A, {"qz", "kin", "vz", "lrs", "zs", "zh", "zl", "eq", "ek", "sml", "kintok", "attm", "kvs", "sps"})
        obs = [A.alloc(f"obs{i}", [128, 4, 512], F32) for i in range(2)]
        qraws = [A.alloc(f"qraw{i}", [128, 2, 512], BF16) for i in range(2)]
        kraws = [A.alloc(f"kraw{i}", [128, 2, 512], BF16) for i in range(2)]
        gla_init(slots, kns, l)
        first = True
        tiles = list(range(NTILE - 1, -1, -1))
        load_xbh(xbhs[0], "xbh0", cur, tiles[0] * TT, False)
        for n_, ti in enumerate(tiles):
            t0 = ti * TT
            seg = ti // TPS
            seg_start = (ti % TPS) == TPS - 1
            link = 1 if (seg == 0 and NSEG > 1) else None
            sl = n_ % 2
            if n_ + 1 < len(tiles):
                load_xbh(xbhs[1 - sl], f"xbh{1 - sl}", cur, tiles[n_ + 1] * TT, False)
            obanks = gla_tile(slots[sl], kns[sl], l, 1, first, seg_start, link, Win, xbhs[sl], f"xbh{sl}",
                              ("save", qraws[sl], kraws[sl], f"qraw{sl}", f"kraw{sl}"))
            first = False
            dma("sp", QSs[:, :, t0:t0 + 512], qraws[sl][:], [f"qraw{sl}"], [("QS", ti)])
            dma("sp", KSs[:, :, t0:t0 + 512], kraws[sl][:], [f"kraw{sl}"], [("KS", ti)])
            vzb = slots[sl][2]
            vsv = VSs[t0:t0 + 512, :].rearrange("(s p) c -> p s c", p=128)
            dma("sp", vsv[0:64], vzb[0:64, :, 0, :], [kns[sl]["vz"]], [("VS", ti, 0)])
            dma("sp", vsv[64:128], vzb[64:128, :, 1, :], [kns[sl]["vz"]], [("VS", ti, 1)])
            for hd in range(4):
                cpany(obs[sl][:, hd, :], ps[obanks[hd]][:, :], [f"ps{obanks[hd]}"], [f"obs{sl}"])
            dma("sp", OB[:, :, t0:t0 + 512], obs[sl][:], [f"obs{sl}"], [("OB", ti)])

    def mixer_pass_f(l, cur):
        A = Arena(P, nc, PBASE, LIMIT, f"mf{l}")
        Win = A.alloc("Win", [128, 8, 3104], BF16, reuse=shared["Win"])
        xbhs = [A.alloc(f"xbh{i}", [128, 8, 514], BF16) for i in range(2)]
        slots, kns = gla_alloc(A, {"qz", "kin", "vz", "sml", "kintok", "attm", "sps"}, reuse=True)
        obl = A.alloc("obl", [128, 4, 512], F32)
        oh = [A.alloc(f"oh{i}", [128, 512], F32) for i in range(2)]
        sqh = [A.alloc(f"sqh{i}", [128, 512], BF16) for i in range(2)]
        rsh = [A.alloc(f"rsh{i}", [128, 512], F32) for i in range(2)]
        sgh = [A.alloc(f"sgh{i}", [128, 512], F32) for i in range(2)]
        ons_ = A.alloc("ons0", [128, 4, 512], BF16)
        ons = [ons_, ons_]
        gcs = [A.alloc(f"gcs{i}", [128, 514], F32) for i in range(2)]
        ux = [A.alloc(f"ux{i}", [128, 514], F32) for i in range(2)]
        cac = [A.alloc(f"cac{i}", [128, 512], F32) for i in range(2)]
        cgs_ = A.alloc("cgs0", [128, 4, 512], BF16)
        cgs = [cgs_, cgs_]
        qraws = [A.alloc(f"qraw{i}", [128, 2, 512], BF16) for i in range(2)]
        kraws = [A.alloc(f"kraw{i}", [128, 2, 512], BF16) for i in range(2)]
        gla_init(slots, kns, l, load=False)

        def load_qkv(ti):
            s2 = ti % 2
            t0_ = ti * TT
            dma("sp", qraws[s2][:], QSs[:, :, t0_:t0_ + 512], [("QS", ti)], [f"qraw{s2}"])
            dma("sp", kraws[s2][:], KSs[:, :, t0_:t0_ + 512], [("KS", ti)], [f"kraw{s2}"])
            vzb = slots[s2][2]
            vsv = VSs[t0_:t0_ + 512, :].rearrange("(s p) c -> p s c", p=128)
            dma("sp", vzb[0:64, :, 0, :], vsv[0:64], [("VS", ti, 0)], [kns[s2]["vz"]])
            dma("sp", vzb[64:128, :, 1, :], vsv[64:128], [("VS", ti, 1)], [kns[s2]["vz"]])

        load_qkv(0)
        load_xbh(xbhs[0], "xbh0", cur, 0, True)
        for ti in range(NTILE):
            t0 = ti * TT
            seg = ti // TPS
            seg_start = (ti % TPS) == 0
            seg_end = (ti % TPS) == TPS - 1
            link = 1 if (seg == 1) else None
            sl = ti % 2
            xbh = xbhs[sl]
            XB = f"xbh{sl}"
            if ti + 1 < NTILE:
                load_xbh(xbhs[1 - sl], f"xbh{1 - sl}", cur, (ti + 1) * TT, True)
                load_qkv(ti + 1)
            dma("sp", obl[:], OB[:, :, t0:t0 + 512], [("OB", ti)], ["obl"])
            obanks = gla_tile(slots[sl], kns[sl], l, 0, ti == 0, seg_start, link, Win, xbh, XB,
                              ("load", qraws[sl], kraws[sl], f"qraw{sl}", f"kraw{sl}"))
            for hd in range(4):
                i2 = hd % 2
                b = obanks[hd]
                tt("dve", oh[i2][:], ps[b][:, :], obl[:, hd, :], ALU.add, [f"ps{b}", "obl"], [f"oh{i2}"])
                act(sqh[i2][:], oh[i2][:], AF.Square, [f"oh{i2}"], [f"sqh{i2}"])
                mm(ps[6][:, :], onesr_b, sqh[i2][:], True, True, [f"sqh{i2}", "cstb"], ["ps6"])
                act(rsh[i2][:], ps[6][:, :], AF.Ln, ["ps6", "cst"], [f"rsh{i2}"], bias=cst[:, K_EPS_RMS:K_EPS_RMS + 1])
                act(rsh[i2][:], rsh[i2][:], AF.Exp, [f"rsh{i2}"], [f"rsh{i2}"], scale=-0.5)
                bgp = 5 if i2 == 0 else 7
                for k in range(8):
                    mm(ps[bgp][:, :], Win[:, k, C_G + hd * 128:C_G + (hd + 1) * 128], xbh[:, k, 0:512], k == 0, k == 7,
                       [WK("Win", k, C_G + hd * 128), XB], [f"ps{bgp}"])
                sigmoid_chain(sgh[i2][:], ps[bgp][:, :], [f"ps{bgp}"], f"sgh{i2}")
                tt("dve", sgh[i2][:], sgh[i2][:], ps[bgp][:, :], ALU.mult, [f"sgh{i2}", f"ps{bgp}"], [f"sgh{i2}"])
                stt(oh[i2][:], oh[i2][:], gng[:, l:l + 1], rsh[i2][:], ALU.mult, ALU.mult,
                    [f"oh{i2}", "gng", f"rsh{i2}"], [f"oh{i2}"])
                tt("dve", ons[sl][:, hd, :], oh[i2][:], sgh[i2][:], ALU.mult, [f"oh{i2}", f"sgh{i2}"], ["ons0"])
            dma("sp", ONs[:, :, t0:t0 + 512], ons[sl][:], ["ons0"], [("ON", ti)])
            fl_l = None if seg_start and seg != 1 else (1 if seg_start else "one")
            fl_r = None if seg_end and seg != 0 else (1 if seg_end else "one")
            if seg_end and seg == 0 and NSEG == 1:
                fl_r = None
            for ch in range(4):
                i2 = ch % 2
                bgc = 0 if i2 == 0 else 2
                bh = 1 if i2 == 0 else 3
                for k in range(8):
                    mm(ps[bgc][:, :], Win[:, k, C_GC + ch * 128:C_GC + (ch + 1) * 128], xbh[:, k, 0:512], k == 0, k == 7,
                       [WK("Win", k, C_GC + ch * 128), XB], [f"ps{bgc}"])
                for k in range(8):
                    mm(ps[7][:, 0:2], Win[:, k, C_GC + ch * 128:C_GC + (ch + 1) * 128], xbh[:, k, 512:514], k == 0, k == 7,
                       [WK("Win", k, C_GC + ch * 128), XB], ["ps7"])
                cp("act", gcs[i2][:, 1:513], ps[bgc][:, :], [f"ps{bgc}"], [f"gcs{i2}"])
                cp("act", gcs[i2][:, 0:1], ps[7][:, 0:1], ["ps7"], [f"gcs{i2}"])
                cp("act", gcs[i2][:, 513:514], ps[7][:, 1:2], ["ps7"], [f"gcs{i2}"])
                for k in range(8):
                    mm(ps[bh][:, :], Win[:, k, C_H + ch * 128:C_H + (ch + 1) * 128], xbh[:, k, 0:512], k == 0, k == 7,
                       [WK("Win", k, C_H + ch * 128), XB], [f"ps{bh}"])
                for k in range(8):
                    mm(ps[7][:, 2:4], Win[:, k, C_H + ch * 128:C_H + (ch + 1) * 128], xbh[:, k, 512:514], k == 0, k == 7,
                       [WK("Win", k, C_H + ch * 128), XB], ["ps7"])
                tt("dve", ux[i2][:, 1:513], ps[bh][:, :], gcs[i2][:, 1:513], ALU.mult, [f"ps{bh}", f"gcs{i2}"], [f"ux{i2}"])
                for (col, pcol, fl) in ((0, 2, fl_l), (513, 3, fl_r)):
                    if fl is None:
                        mset("dve", ux[i2][:, col:col + 1], 0.0, [f"ux{i2}"])
                    elif fl == "one":
                        tt("dve", ux[i2][:, col:col + 1], ps[7][:, pcol:pcol + 1], gcs[i2][:, col:col + 1], ALU.mult,
                           ["ps7", f"gcs{i2}"], [f"ux{i2}"])
                    else:
                        stt(ux[i2][:, col:col + 1], ps[7][:, pcol:pcol + 1], flags[:, 1:2], gcs[i2][:, col:col + 1],
                            ALU.mult, ALU.mult, ["ps7", f"gcs{i2}", "flags"], [f"ux{i2}"])
                w0 = convw[:, (l * 3 + 0) * 4 + ch:(l * 3 + 0) * 4 + ch + 1]
                w1 = convw[:, (l * 3 + 1) * 4 + ch:(l * 3 + 1) * 4 + ch + 1]
                w2 = convw[:, (l * 3 + 2) * 4 + ch:(l * 3 + 2) * 4 + ch + 1]
                act(cac[i2][:], ux[i2][:, 1:513], AF.Copy, [f"ux{i2}", "convw"], [f"cac{i2}"], scale=w1)
                stt(cac[i2][:], ux[i2][:, 0:512], w0, cac[i2][:], ALU.mult, ALU.add, [f"ux{i2}", "convw", f"cac{i2}"], [f"cac{i2}"])
                stt(cac[i2][:], ux[i2][:, 2:514], w2, cac[i2][:], ALU.mult, ALU.add, [f"ux{i2}", "convw", f"cac{i2}"], [f"cac{i2}"])
                for k in range(8):
                    mm(ps[bgc][:, :], Win[:, k, C_GB + ch * 128:C_GB + (ch + 1) * 128], xbh[:, k, 0:512], k == 0, k == 7,
                       [WK("Win", k, C_GB + ch * 128), XB], [f"ps{bgc}"])
                tt("dve", cgs[sl][:, ch, :], cac[i2][:], ps[bgc][:, :], ALU.mult, [f"cac{i2}", f"ps{bgc}"], ["cgs0"])
            dma("sp", CGs[:, :, t0:t0 + 512], cgs[sl][:], ["cgs0"], [("CG", ti)])

    def mixer_pass_g(l, cur):
        A = Arena(P, nc, PBASE, LIMIT, f"mg{l}")
        Wm = A.alloc("Wm", [128, 8, 2048], BF16)
        load_w(A, Wm, w_in[l].rearrange("(k p) f -> p k f", p=128)[:, :, C_MC:DIN], "Wm", 2048, order=[0, 2, 1, 3])
        Wco = A.alloc("Wco", [128, 4, D], BF16)
        Wgo = A.alloc("Wgo", [128, 4, D], BF16)
        Wmo = A.alloc("Wmo", [128, 8, D], BF16)
        wmo_off = A.last_off
        xb = [A.alloc(f"xb{i}", [128, 8, 512], BF16) for i in range(2)]
        xfs = [A.alloc(f"xf{i}", [128, 8, 512], F32, split=8) for i in range(2)]
        vb = A.alloc("vb", [128, 8, 512], BF16, split=8)
        onl = A.alloc("onl", [128, 4, 512], BF16)
        cgl = A.alloc("cgl", [128, 4, 512], BF16)
        sg = [A.alloc(f"sg{i}", [128, 512], F32) for i in range(2)]
        sc = [A.alloc(f"sc{i}", [128, 512], F32) for i in range(2)]
        t1 = A.alloc("t1", [128, 512], F32)
        t2 = A.alloc("t2", [128, 512], F32)
        mrg = A.alloc("mrg", [128, 8, 512], BF16, split=8)
        sqb = [A.alloc(f"sqb{i}", [128, 512], BF16) for i in range(2)]
        lt = A.alloc("lt", [128, 512], F32)
        rstd = A.alloc("rstd", [128, 512], F32)
        dma("pool", Wco[:], w_co[l].rearrange("(f p) d -> p f d", p=128), [], ["Wco"], max_dma_last_dim=8192)
        dma("pool", Wgo[:], w_go[l].rearrange("(f p) d -> p f d", p=128), [], ["Wgo"], max_dma_last_dim=8192)
        A.last_off = wmo_off
        load_w(A, Wmo, w_mo[l].rearrange("(k p) d -> p k d", p=128), "Wmo", D)

        def load(ti):
            t0 = ti * TT
            dma("pool", xb[ti % 2][:], Xv[cur][:, :, t0:t0 + 512], xkeys(cur, t0, 512), [f"xb{ti % 2}"])

        load(0)
        for ti in range(NTILE):
            t0 = ti * TT
            sl = ti % 2
            if ti + 1 < NTILE:
                load(ti + 1)
            dma("sp", onl[:], ONs[:, :, t0:t0 + 512], [("ON", ti)], ["onl"])
            dma("sp", cgl[:], CGs[:, :, t0:t0 + 512], [("CG", ti)], ["cgl"])
            xf = xfs[sl]
            XF = f"xf{sl}"
            dma("sp", xf[:], Xv[cur][:, :, t0:t0 + 512], xkeys(cur, t0, 512), [XF])
            for d in range(8):
                i2 = d % 2
                for k in range(8):
                    mm(ps[0][:, :], Wm[:, k, d * 128:(d + 1) * 128], xb[sl][:, k, :], k == 0, k == 7,
                       [WK("Wm", k, d * 128), f"xb{sl}"], ["ps0"])
                sigmoid_chain(sc[i2][:], ps[0][:, :], ["ps0"], f"sc{i2}")
                for k in range(8):
                    mm(ps[1][:, :], Wm[:, k, 1024 + d * 128:1024 + (d + 1) * 128], xb[sl][:, k, :], k == 0, k == 7,
                       [WK("Wm", k, 1024 + d * 128), f"xb{sl}"], ["ps1"])
                sigmoid_chain(sg[i2][:], ps[1][:, :], ["ps1"], f"sg{i2}")
                for f in range(4):
                    mm(ps[2][:, :], Wco[:, f, d * 128:(d + 1) * 128], cgl[:, f, :], f == 0, f == 3, ["Wco", "cgl"], ["ps2"])
                for f in range(4):
                    mm(ps[3][:, :], Wgo[:, f, d * 128:(d + 1) * 128], onl[:, f, :], f == 0, f == 3, ["Wgo", "onl"], ["ps3"])
                tt("dve", t1[:], sc[i2][:], ps[2][:, :], ALU.mult, [f"sc{i2}", "ps2"], ["t1"])
                tt("dve", t2[:], sg[i2][:], ps[3][:, :], ALU.mult, [f"sg{i2}", "ps3"], ["t2"])
                tt("dve", mrg[:, d, :], t1[:], t2[:], ALU.add, ["t1", "t2"], [f"mrg.{d}"])
            for d in range(8):
                b = 4 + d % 2
                for k in range(8):
                    mm(ps[b][:, :], Wmo[:, k, d * 128:(d + 1) * 128], mrg[:, k, :], k == 0, k == 7,
                       [WK("Wmo", k, d * 128), f"mrg.{k}"], [f"ps{b}"])
                stt(xf[:, d, :], xf[:, d, :], ALPHA, ps[b][:, :], ALU.mult, ALU.add, [f"ps{b}", f"{XF}.{d}"], [f"{XF}.{d}"])
            ln_epilogue(xf, XF, 512, l, 1, vb, sqb, lt, rstd, 6, 7, Xv[1 - cur], t0, xkeys(1 - cur, t0, 512))

    def xattn_phase(l, cur):
        A = Arena(P, nc, PBASE, LIMIT, f"xa{l}")
        Wq = A.alloc("Wq", [128, 8, D], BF16)
        wq_off = A.last_off
        Wkv = A.alloc("Wkv", [128, 8, 2 * D], BF16)
        load_w(A, Wkv, w_xkv[l].rearrange("(k p) d -> p k d", p=128), "Wkv", 2 * D)
        A.last_off = wq_off
        load_w(A, Wq, w_xq[l].rearrange("(k p) d -> p k d", p=128), "Wq", D)
        Wo = A.alloc("Wo", [128, 8, D], BF16)
        load_w(A, Wo, w_xo[l].rearrange("(k p) d -> p k d", p=128), "Wo", D)
        memT = A.alloc("memT", [128, 8, NMEM], BF16)
        KT = [A.alloc(f"KT{s}", [128, 8, NMEM], BF16) for s in range(NSEG)]
        Vt = [A.alloc(f"Vt{s}", [128, 2, D], BF16) for s in range(NSEG)]
        xb = [A.alloc(f"xb{i}", [128, 8, 512], BF16) for i in range(2)]
        xfs = [A.alloc(f"xf{i}", [128, 8, 512], F32, split=8) for i in range(2)]
        vb = A.alloc("vb", [128, 8, 512], BF16, split=8)
        qs = A.alloc("qs", [128, 8, 512], BF16, split=8)
        pT = [A.alloc(f"pT{i}", [128, 2, 512], BF16) for i in range(2)]
        rden = [A.alloc(f"rden{i}", [128, 512], F32) for i in range(2)]
        oa = A.alloc("oa", [128, 8, 512], BF16, split=8)
        sqb = [A.alloc(f"sqb{i}", [128, 512], BF16) for i in range(2)]
        lt = A.alloc("lt", [128, 512], F32)
        rstd = A.alloc("rstd", [128, 512], F32)
        for sgm in range(NSEG):
            dma("pool", memT[:], MEMT[:, :, sgm * NMEM:(sgm + 1) * NMEM], [("MEMT", 2 * sgm), ("MEMT", 2 * sgm + 1)], ["memT"])
            for dch in range(8):
                b = dch % 2
                for k in range(8):
                    mm(ps[b][:, 0:NMEM], Wkv[:, k, dch * 128:(dch + 1) * 128], memT[:, k, :], k == 0, k == 7,
                       [WK("Wkv", k, dch * 128), "memT"], [f"ps{b}"])
                cpany(KT[sgm][:, dch, :], ps[b][:, 0:NMEM], [f"ps{b}"], [f"KT{sgm}"])
            for mch in range(2):
                for hf in range(2):
                    b = 2 + hf
                    for k in range(8):
                        mm(ps[b][:, :], memT[:, k, mch * 128:(mch + 1) * 128], Wkv[:, k, D + hf * 512:D + (hf + 1) * 512],
                           k == 0, k == 7, [WK("Wkv", k, D + hf * 512), "memT"], [f"ps{b}"])
                    cpany(Vt[sgm][:, mch, hf * 512:(hf + 1) * 512], ps[b][:, :], [f"ps{b}"], [f"Vt{sgm}"])

        def load(ti):
            t0 = ti * TT
            dma("pool", xb[ti % 2][:], Xv[cur][:, :, t0:t0 + 512], xkeys(cur, t0, 512), [f"xb{ti % 2}"])

        load(0)
        for ti in range(NTILE):
            t0 = ti * TT
            sl = ti % 2
            sgm = ti // TPS
            if ti + 1 < NTILE:
                load(ti + 1)
            xf = xfs[sl]
            XF = f"xf{sl}"
            dma("sp", xf[:], Xv[cur][:, :, t0:t0 + 512], xkeys(cur, t0, 512), [XF])
            for dch in range(8):
                b = dch % 2
                for k in range(8):
                    mm(ps[b][:, :], Wq[:, k, dch * 128:(dch + 1) * 128], xb[sl][:, k, :], k == 0, k == 7,
                       [WK("Wq", k, dch * 128), f"xb{sl}"], [f"ps{b}"])
                cpany(qs[:, dch, :], ps[b][:, :], [f"ps{b}"], [f"qs.{dch}"])
            for hd in range(4):
                i2 = hd % 2
                for mch in range(2):
                    b = 2 + mch
                    for dd in range(2):
                        dch = 2 * hd + dd
                        mm(ps[b][:, :], KT[sgm][:, dch, mch * 128:(mch + 1) * 128], qs[:, dch, :], dd == 0, dd == 1,
                           [f"KT{sgm}", f"qs.{dch}"], [f"ps{b}"])
                    act(pT[i2][:, mch, :], ps[b][:, :], AF.Exp, [f"ps{b}"], [f"pT{i2}"], scale=1.0 / 16.0)
                for mch in range(2):
                    mm(ps[4][:, :], onesb, pT[i2][:, mch, :], mch == 0, mch == 1, ["cstb", f"pT{i2}"], ["ps4"])
                act(rden[i2][:], ps[4][:, :], AF.Ln, ["ps4"], [f"rden{i2}"])
                act(rden[i2][:], rden[i2][:], AF.Exp, [f"rden{i2}"], [f"rden{i2}"], scale=-1.0)
                for dvc in range(2):
                    b = 5 + dvc
                    for mch in range(2):
                        mm(ps[b][:, :], Vt[sgm][:, mch, hd * 256 + dvc * 128:hd * 256 + (dvc + 1) * 128], pT[i2][:, mch, :],
                           mch == 0, mch == 1, [f"Vt{sgm}", f"pT{i2}"], [f"ps{b}"])
                    tt("dve", oa[:, 2 * hd + dvc, :], ps[b][:, :], rden[i2][:], ALU.mult, [f"ps{b}", f"rden{i2}"],
                       [f"oa.{2 * hd + dvc}"])
            for d in range(8):
                b = d % 2
                for k in range(8):
                    mm(ps[b][:, :], Wo[:, k, d * 128:(d + 1) * 128], oa[:, k, :], k == 0, k == 7,
                       [WK("Wo", k, d * 128), f"oa.{k}"], [f"ps{b}"])
                stt(xf[:, d, :], xf[:, d, :], ALPHA, ps[b][:, :], ALU.mult, ALU.add, [f"ps{b}", f"{XF}.{d}"], [f"{XF}.{d}"])
            ln_epilogue(xf, XF, 512, l, 2, vb, sqb, lt, rstd, 6, 7, Xv[1 - cur], t0, xkeys(1 - cur, t0, 512))

    cur = 0
    P.marks = [("pro", 0)]
    for l in range(L):
        P.marks.append((f"ffn1.{l}", len(P.ops)))
        ffn_phase(l, 0, w_gu1, w_d1, cur); cur = 1 - cur
        P.marks.append((f"passB.{l}", len(P.ops)))
        mixer_pass_b(l, cur)
        P.marks.append((f"passF.{l}", len(P.ops)))
        mixer_pass_f(l, cur)
        P.marks.append((f"passG.{l}", len(P.ops)))
        mixer_pass_g(l, cur); cur = 1 - cur
        P.marks.append((f"xattn.{l}", len(P.ops)))
        xattn_phase(l, cur); cur = 1 - cur
        P.marks.append((f"ffn2.{l}", len(P.ops)))
        ffn_phase(l, 3, w_gu2, w_d2, cur); cur = 1 - cur
    P.marks.append(("epi", len(P.ops)))

    A = Arena(P, nc, PBASE, LIMIT, "tout")
    xt = [A.alloc(f"xt{i}", [128, 8, 256], F32) for i in range(2)]
    ys = [A.alloc(f"ys{i}", [128, 2, D], F32) for i in range(2)]
    outs = []
    for ti in range(NT // 256):
        t0 = ti * 256
        sl = ti % 2
        dma("sp", xt[sl][:], Xv[cur][:, :, t0:t0 + 256], xkeys(cur, t0, 256), [f"xt{sl}"])
        for s in range(2):
            for k in range(8):
                b = (s * 2 + k // 4)
                tr(ps[b][:, (k % 4) * 128:(k % 4 + 1) * 128], xt[sl][:, k, s * 128:(s + 1) * 128], ident,
                   [f"xt{sl}", "cst"], [f"ps{b}"])
            for hf in range(2):
                b = s * 2 + hf
                cpany(ys[sl][:, s, hf * 512:(hf + 1) * 512], ps[b][:, :], [f"ps{b}"], [f"ys{sl}"])
        outs.append(dma("sp", y_out[t0:t0 + 256, :].rearrange("(s p) f -> p s f", p=128), ys[sl][:], [f"ys{sl}"],
                        [("Y", ti)]))
    P.finalize(outs)

    semnames = ["pe", "act", "dve", "pool"] + [(q, i) for q in ("sp", "pool", "act") for i in range(NDMASEM)]
    from contextlib import ExitStack
    with ExitStack() as st:
        sems = {}
        for k in semnames:
            nm = k if isinstance(k, str) else f"d{k[0]}{k[1]}"
            sems[k] = st.enter_context(nc.semaphore("s_" + nm))
        block = st.enter_context(nc.Block())
        P.emit(block, sems)
    nc._prog_stats = (len(P.ops), P.nwaits)
    nc._sim = (P.makespan, P.busy)
    nc._P = P
    return nc


def prep_shared(inputs, L=4):
    sh = {}
    for k in ("ffn1_w_gu", "ffn1_w_down", "ffn2_w_gu", "ffn2_w_down", "w_mix_in", "w_conv_out", "w_gla_out",
              "w_mix_out", "gla_gate_w2", "xa_w_q", "xa_w_kv", "xa_w_o"):
        sh[k] = np.ascontiguousarray(np.asarray(inputs[k], dtype=np.float32))
    sh["consts"] = make_consts()
    ln_g = np.asarray(inputs["ln_g"], np.float32); ln_b = np.asarray(inputs["ln_b"], np.float32)
    lnp = np.zeros((128, L * 64), np.float32)
    for l in range(L):
        for j in range(4):
            for gb, arr in enumerate((ln_g, ln_b)):
                base = ((l * 4 + j) * 2 + gb) * 8
                lnp[:, base:base + 8] = arr[l, j].reshape(8, 128).T
    sh["lnp"] = lnp
    cw = np.asarray(inputs["conv_w"], np.float32)
    convw = np.zeros((128, L * 12), np.float32)
    for l in range(L):
        for tap in range(3):
            convw[:, (l * 3 + tap) * 4:(l * 3 + tap) * 4 + 4] = cw[l, tap].reshape(4, 128).T
    sh["convw"] = convw
    sh["gng"] = np.ascontiguousarray(np.asarray(inputs["gla_norm_g"], np.float32)[:L].T)
    gb = np.asarray(inputs["gla_gate_b"], np.float32)[:L]
    sh["gbias"] = np.ascontiguousarray(np.broadcast_to(gb.reshape(1, L * 512), (128, L * 512)))
    return sh


_NC_CACHE = {}


def kernel(**inputs):
    L = 4
    xp = np.asarray(inputs["x_prompt"], np.float32)
    xs = np.asarray(inputs["x_sample"], np.float32)
    mp = np.asarray(inputs["mem_prompt"], np.float32)
    ms = np.asarray(inputs["mem_sample"], np.float32)
    sh = prep_shared(inputs, L)
    in_maps = []
    for c in range(8):
        m = dict(sh)
        if c < 4:
            x = np.concatenate([xs[c], xp[c]], axis=0)
            mem = np.concatenate([ms[c], ms[c], mp[c]], axis=0)
            link = 1.0
        else:
            i0 = 4 + (c - 4) * 3
            x = np.concatenate([xp[i0], xp[i0 + 1], xp[i0 + 2]], axis=0)
            mem = np.concatenate([mp[i0], mp[i0 + 1], mp[i0 + 2]], axis=0)
            link = 0.0
        fl = np.zeros((128, 2), np.float32)
        fl[:, 1] = link
        m["x_in"] = np.ascontiguousarray(x)
        m["mem_in"] = np.ascontiguousarray(mem)
        m["flags"] = fl
        in_maps.append(m)
    if "nc" not in _NC_CACHE:
        _NC_CACHE["nc"] = build(L=L)
    nc = _NC_CACHE["nc"]
    res = run_bass_kernel_spmd(nc, in_maps, core_ids=list(range(8)))
    y_prompt = np.zeros_like(xp)
    y_sample = np.zeros_like(xs)
    for c in range(8):
        y = np.asarray(res.results[c]["y_out"], np.float32)
        if c < 4:
            y_sample[c] = y[0:4096]
            y_prompt[c] = y[4096:6144]
        else:
            i0 = 4 + (c - 4) * 3
            for s in range(3):
                y_prompt[i0 + s] = y[s * 2048:(s + 1) * 2048]
    return (y_prompt, y_sample)
```

```python
import numpy as np
import concourse.bass as bass
import concourse.mybir as mybir
from concourse.bass_utils import run_bass_kernel_spmd

F32 = mybir.dt.float32
BF16 = mybir.dt.bfloat16
AF = mybir.ActivationFunctionType
ALU = mybir.AluOpType

D = 1024
DFF = 2816
DIN = 5152
NMEM = 256
ALPHA = float((2 * 4) ** 0.25)
SLAB = 256
NDMASEM = 8

C_H, C_GB, C_GC, C_Q, C_K, C_V, C_G, C_LRF, C_LRB, C_MC, C_MG = (
    0, 512, 1024, 1536, 1792, 2048, 2560, 3072, 3088, 3104, 4128)

K_IDENT = 0
K_ONESM = 128
K_ONESR = 256
K_TRID_F = 384
K_TRID_B = 512
K_MASK_F = 640
K_MASK_B = 768
K_TRIX_F = 896
K_TRIX_B = 900
K_EPS_LN = 904
K_EPS_RMS = 905
K_ONE = 906
K_ONES1 = 907
NCONST = 907 + 128


def make_consts():
    c = np.zeros((128, NCONST), np.float32)
    c[:, K_IDENT:K_IDENT + 128] = np.eye(128, dtype=np.float32)
    c[:, K_ONESM:K_ONESM + 128] = 1.0 / 1024.0
    c[:, K_ONESR:K_ONESR + 128] = 1.0 / 128.0
    j = np.arange(128)[:, None]
    i = np.arange(128)[None, :]
    same = (j // 64) == (i // 64)
    jj = j % 64
    ii = i % 64
    s = -1.0 / 16.0
    c[:, K_TRID_F:K_TRID_F + 128] = s * (same * ((jj <= ii).astype(np.float32) - (jj <= 31).astype(np.float32)))
    c[:, K_TRID_B:K_TRID_B + 128] = s * (same * ((jj >= ii).astype(np.float32) - (jj >= 32).astype(np.float32)))
    c[:, K_MASK_F:K_MASK_F + 128] = (same & (jj <= ii)).astype(np.float32)
    c[:, K_MASK_B:K_MASK_B + 128] = (same & (jj > ii)).astype(np.float32)
    jv = np.arange(128)
    for col in range(2):
        inch = (jv // 64) == col
        jl = jv % 64
        c[:, K_TRIX_F + col] = s * (inch & (jl <= 31))
        c[:, K_TRIX_F + 2 + col] = s * (inch & (jl >= 32))
        c[:, K_TRIX_B + col] = s * (inch & (jl >= 32))
        c[:, K_TRIX_B + 2 + col] = s * (inch & (jl <= 31))
    c[:, K_EPS_LN] = 1e-5
    c[:, K_EPS_RMS] = 1e-6
    c[:, K_ONE] = 1.0
    c[:, K_ONES1:K_ONES1 + 128] = 1.0
    return c


class Op:
    __slots__ = ("eng", "fn", "deps", "odeps", "sig", "semkey", "semval", "is_dma", "waits", "vc", "cost", "nbytes", "pri")


LAT_X = 0.45


class Prog:
    def __init__(self, nc):
        self.nc = nc
        self.ops = []
        self.last_w = {}
        self.readers = {}
        self.alias = {}
        self.shift = 0

    def reg(self, name, off, nbytes):
        self.alias[name] = tuple(("sb", i) for i in range(off // SLAB, (off + nbytes - 1) // SLAB + 1))

    def expand(self, keys):
        out = []
        for k in keys:
            a = self.alias.get(k)
            if a is None:
                out.append(k)
            else:
                out.extend(a)
        return out

    def op(self, eng, fn, r=(), w=(), dma=False, cost=0.3, nbytes=0):
        i = len(self.ops)
        o = Op()
        o.eng = eng
        o.fn = fn
        o.is_dma = dma
        o.sig = dma
        o.semkey = None
        o.semval = 0
        o.waits = None
        o.vc = None
        o.cost = cost
        o.nbytes = nbytes
        o.pri = i + self.shift
        rk = self.expand(r)
        wk = self.expand(w)
        deps = set()
        for k in rk:
            lw = self.last_w.get(k)
            if lw is not None:
                deps.add(lw)
        for k in wk:
            lw = self.last_w.get(k)
            if lw is not None:
                deps.add(lw)
            rd = self.readers.get(k)
            if rd:
                deps.update(rd)
        for k in rk:
            self.readers.setdefault(k, []).append(i)
        for k in wk:
            self.last_w[k] = i
            self.readers[k] = []
        if eng == "pe" and not dma:
            o.odeps = {d for d in deps if (self.ops[d].eng == "pe" and not self.ops[d].is_dma)}
            o.deps = deps - o.odeps
        else:
            o.odeps = set()
            o.deps = deps
        self.ops.append(o)
        return i

    def schedule(self):
        import heapq
        ops = self.ops
        n = len(ops)
        succ = [[] for _ in range(n)]
        indeg = [0] * n
        for i, o in enumerate(ops):
            for d in o.deps:
                succ[d].append(i)
            for d in o.odeps:
                succ[d].append(i)
            indeg[i] = len(o.deps) + len(o.odeps)
        engs = ["pe", "act", "dve", "pool", "sp"]
        free = {e: 0.0 for e in engs}
        avail = {e: [] for e in engs}
        pend = {e: [] for e in engs}
        rtime = [0.0] * n
        fin = [0.0] * n
        for i, o in enumerate(ops):
            if indeg[i] == 0:
                heapq.heappush(pend[o.eng], (0.0, o.pri, i))
        dma_free = 0.0
        order = []
        while len(order) < n:
            best = None
            for e in engs:
                ft = free[e]
                pe_ = pend[e]
                av = avail[e]
                while pe_ and pe_[0][0] <= ft:
                    _, pr_, i = heapq.heappop(pe_)
                    heapq.heappush(av, (pr_, i))
                if av:
                    cand = (ft, av[0][0], av[0][1], e, True)
                elif pe_:
                    cand = (pe_[0][0], pe_[0][1], pe_[0][2], e, False)
                else:
                    continue
                if best is None or (cand[0], cand[1], cand[2]) < (best[0], best[1], best[2]):
                    best = cand
            st, _pr, i, e, fa = best
            if fa:
                heapq.heappop(avail[e])
            else:
                heapq.heappop(pend[e])
            o = ops[i]
            if o.is_dma:
                issue = 0.25 if e == "sp" else 0.6
                free[e] = st + issue
                t0 = max(st + issue, dma_free)
                xfer = o.nbytes / 160000.0
                dma_free = t0 + xfer
                fin[i] = t0 + xfer + 2.0
            else:
                free[e] = st + o.cost
                fin[i] = st + o.cost
            order.append(i)
            for s_ in succ[i]:
                so = ops[s_]
                lat = 0.0 if (so.eng == "pe" and e == "pe" and not o.is_dma and not so.is_dma) else LAT_X
                t = fin[i] + lat
                if t > rtime[s_]:
                    rtime[s_] = t
                indeg[s_] -= 1
                if indeg[s_] == 0:
                    heapq.heappush(pend[so.eng], (rtime[s_], so.pri, s_))
        self.order = order
        self.makespan = max(fin)
        self.fin = fin
        self.busy = {e: 0.0 for e in engs}
        for o in ops:
            if not o.is_dma:
                self.busy[o.eng] += o.cost

    def finalize(self, final_wait_ops):
        ops = self.ops
        self.schedule()
        order = self.order
        self.eng_ops = {"pe": [], "act": [], "dve": [], "pool": [], "sp": []}
        for i in order:
            self.eng_ops[ops[i].eng].append(i)
        dcount = {}
        slot_last = {}
        for i in order:
            o = ops[i]
            if o.is_dma:
                c = dcount.get(o.eng, 0)
                dcount[o.eng] = c + 1
                o.semkey = (o.eng, c % NDMASEM)
                prev = slot_last.get(o.semkey)
                if prev is not None:
                    o.deps = set(o.deps)
                    o.deps.add(prev)
                slot_last[o.semkey] = i
        fin = Op()
        fin.eng = "sp"; fin.fn = None; fin.is_dma = False; fin.sig = False
        fin.semkey = None; fin.semval = 0; fin.waits = None; fin.vc = None
        fin.deps = set(final_wait_ops); fin.odeps = set(); fin.cost = 0; fin.nbytes = 0; fin.pri = 1 << 60
        ops.append(fin)
        fi = len(ops) - 1
        order = order + [fi]
        self.eng_ops["sp"].append(fi)
        for o in ops:
            for d in o.deps:
                ops[d].sig = True
        cnt = {}
        slotcnt = {}
        for i in order:
            o = ops[i]
            if not o.sig:
                continue
            if o.is_dma:
                slotcnt[o.semkey] = slotcnt.get(o.semkey, 0) + 16
                o.semval = slotcnt[o.semkey]
            else:
                o.semkey = o.eng
                cnt[o.eng] = cnt.get(o.eng, 0) + 1
                o.semval = cnt[o.eng]
        clock = {e: {} for e in self.eng_ops}
        nwaits = 0
        for i in order:
            o = ops[i]
            ck = clock[o.eng]
            waits = {}
            for d in o.deps:
                dd = ops[d]
                if ck.get(dd.semkey, 0) >= dd.semval:
                    continue
                if waits.get(dd.semkey, 0) < dd.semval:
                    waits[dd.semkey] = dd.semval
            for d in o.deps:
                dd = ops[d]
                for k, v in dd.vc.items():
                    if ck.get(k, 0) < v:
                        ck[k] = v
            o.waits = list(waits.items())
            nwaits += len(o.waits)
            if o.sig:
                vc = dict(ck)
                vc[o.semkey] = o.semval
                o.vc = vc
        self.nwaits = nwaits

    def emit(self, block, sems):
        ops = self.ops

        def run(eng_name, e):
            for i in self.eng_ops[eng_name]:
                o = ops[i]
                for (k, v) in o.waits:
                    e.wait_ge(sems[k], v)
                if o.fn is None:
                    continue
                ins = o.fn(e)
                if o.sig:
                    ins.then_inc(sems[o.semkey], 16 if o.is_dma else 1)

        @block.sync
        def _(e):
            run("sp", e)

        @block.scalar
        def _(e):
            run("act", e)

        @block.vector
        def _(e):
            run("dve", e)

        @block.gpsimd
        def _(e):
            run("pool", e)

        @block.tensor
        def _(e):
            run("pe", e)


class Arena:
    def __init__(self, P, nc, base, limit, tag):
        self.P = P; self.nc = nc; self.off = base; self.limit = limit; self.tag = tag; self.n = 0

    def alloc(self, name, shape, dtype, split=None, reuse=None):
        esz = 4 if dtype == F32 else 2
        per = 1
        for s in shape[1:]:
            per *= s
        nbytes = per * esz
        self.off = (self.off + SLAB - 1) // SLAB * SLAB
        if reuse is not None:
            assert reuse[1] == self.off, (name, reuse[1], self.off)
            h = reuse[0]
        else:
            h = self.nc.alloc_sbuf_tensor_at(f"{self.tag}_{name}", list(shape), dtype, offset=self.off)
        self.last_off = self.off
        self.P.reg(name, self.off, nbytes)
        if split:
            sub = nbytes // split
            for i in range(split):
                self.P.reg(f"{name}.{i}", self.off + i * sub, sub)
        self.off += nbytes
        assert self.off <= self.limit, (self.tag, name, self.off, self.limit)
        return h


def build(L=4, NSEG=3, SEGLEN=2048, TTF=384):
    NT = NSEG * SEGLEN
    TT = 512
    nc = bass.Bass("TRN2", target_bir_lowering=False)
    P = Prog(nc)

    def din(name, shape, dt=F32):
        return nc.dram_tensor(name, list(shape), dt, kind="ExternalInput").ap()

    x_in = din("x_in", [NT, D])
    mem_in = din("mem_in", [NSEG * NMEM, D])
    w_gu1 = din("ffn1_w_gu", [L, D, 2 * DFF]); w_d1 = din("ffn1_w_down", [L, DFF, D])
    w_gu2 = din("ffn2_w_gu", [L, D, 2 * DFF]); w_d2 = din("ffn2_w_down", [L, DFF, D])
    w_in = din("w_mix_in", [L, D, DIN])
    w_co = din("w_conv_out", [L, 512, D]); w_go = din("w_gla_out", [L, 512, D])
    w_mo = din("w_mix_out", [L, D, D])
    w_g2 = din("gla_gate_w2", [L, 2, 16, 256])
    w_xq = din("xa_w_q", [L, D, D]); w_xkv = din("xa_w_kv", [L, D, 2 * D]); w_xo = din("xa_w_o", [L, D, D])
    consts_d = din("consts", [128, NCONST])
    flags_d = din("flags", [128, 2])
    lnp_d = din("lnp", [128, L * 64])
    convw_d = din("convw", [128, L * 12])
    gng_d = din("gng", [128, L])
    gbias_d = din("gbias", [128, L * 512])
    y_out = nc.dram_tensor("y_out", [NT, D], F32, kind="ExternalOutput").ap()

    X = [nc.dram_tensor(f"Xs{i}", [D, NT], F32, kind="Internal").ap() for i in range(2)]
    Xv = [x.rearrange("(k p) t -> p k t", p=128) for x in X]
    OB = nc.dram_tensor("OBs", [512, NT], F32, kind="Internal").ap().rearrange("(k p) t -> p k t", p=128)
    ONs = nc.dram_tensor("ONs", [512, NT], BF16, kind="Internal").ap().rearrange("(k p) t -> p k t", p=128)
    CGs = nc.dram_tensor("CGs", [512, NT], BF16, kind="Internal").ap().rearrange("(k p) t -> p k t", p=128)
    QSs = nc.dram_tensor("QSs", [256, NT], BF16, kind="Internal").ap().rearrange("(k p) t -> p k t", p=128)
    KSs = nc.dram_tensor("KSs", [256, NT], BF16, kind="Internal").ap().rearrange("(k p) t -> p k t", p=128)
    VSs = nc.dram_tensor("VSs", [NT, 512], BF16, kind="Internal").ap()
    MEMT = nc.dram_tensor("MEMTs", [D, NSEG * NMEM], F32, kind="Internal").ap().rearrange("(k p) t -> p k t", p=128)

    SB0 = (nc.SBUF_PARTITION_SIZE_BYTES - nc.sbuf_bytes_remaining + 255) // 256 * 256
    LIMIT = nc.SBUF_PARTITION_SIZE_BYTES - 256
    AP_ = Arena(P, nc, SB0, LIMIT, "g")
    cst = AP_.alloc("cst", [128, NCONST], F32)
    cstb = AP_.alloc("cstb", [128, NCONST], BF16)
    flags = AP_.alloc("flags", [128, 2], F32)
    lnp = AP_.alloc("lnp", [128, L * 64], F32)
    convw = AP_.alloc("convw", [128, L * 12], F32)
    gng = AP_.alloc("gng", [128, L], F32)
    Sst = AP_.alloc("Sst", [128, 2, 128], F32, split=2)
    PBASE = (AP_.off + 1023) // 1024 * 1024

    ps = [nc.alloc_psum_tensor(f"psb{i}", [128, 512], F32) for i in range(8)]

    ident = cst[:, K_IDENT:K_IDENT + 128]
    ones_m = cst[:, K_ONESM:K_ONESM + 128]
    ones_r = cst[:, K_ONESR:K_ONESR + 128]
    identb = cstb[:, K_IDENT:K_IDENT + 128]
    onesb = cstb[:, K_ONES1:K_ONES1 + 128]
    onesm_b = cstb[:, K_ONESM:K_ONESM + 128]
    onesr_b = cstb[:, K_ONESR:K_ONESR + 128]

    def xkeys(w, t0, n):
        return [("X", w, b) for b in range(t0 // 128, (t0 + n + 127) // 128)]

    def fsz(ap):
        n_ = 1
        for v in ap.shape[1:]:
            n_ *= v
        return n_

    def mm(out, lhsT, rhs, start, stop, r, w):
        nn = fsz(rhs)
        c = max(nn / 2300.0, 0.095)
        if lhsT.dtype == F32:
            c = c * 4.5 + 0.2
        return P.op("pe", lambda e: e.matmul(out, lhsT, rhs, start=start, stop=stop), r=r, w=w, cost=c)

    def tr(out, in_, idn, r, w):
        c = 0.1 if in_.dtype == BF16 else 0.22
        return P.op("pe", lambda e: e.transpose(out, in_, idn), r=r, w=w, cost=c)

    def act(out, in_, func, r, w, bias=None, scale=None):
        kw = {}
        if bias is not None:
            kw["bias"] = bias
        if scale is not None:
            kw["scale"] = scale
        c = 0.22 + fsz(out) / 1400.0
        return P.op("act", lambda e: e.activation(out, in_, func, **kw), r=r, w=w, cost=c)

    def vcost(eng, out):
        if eng == "pool":
            return 0.3 + fsz(out) / 480.0
        return 0.1 + fsz(out) / 960.0

    def tt(eng, out, in0, in1, op, r, w):
        return P.op(eng, lambda e: e.tensor_tensor(out, in0, in1, op), r=r, w=w, cost=vcost(eng, out))

    def ts(eng, out, in0, s1, s2, op0, op1, r, w):
        if s2 is None:
            return P.op(eng, lambda e: e.tensor_scalar(out, in0, s1, None, op0), r=r, w=w, cost=vcost(eng, out))
        return P.op(eng, lambda e: e.tensor_scalar(out, in0, s1, s2, op0, op1), r=r, w=w, cost=vcost(eng, out))

    def stt(out, in0, sc, in1, op0, op1, r, w):
        return P.op("dve", lambda e: e.scalar_tensor_tensor(out, in0, sc, in1, op0, op1), r=r, w=w,
                    cost=vcost("dve", out))

    def cp(eng, out, in_, r, w):
        if eng == "act":
            return P.op("act", lambda e: e.copy(out, in_), r=r, w=w, cost=0.22 + fsz(out) / 1400.0)
        return P.op(eng, lambda e: e.tensor_copy(out, in_), r=r, w=w, cost=vcost(eng, out))

    def amul(out, in_, val, r, w):
        return P.op("act", lambda e: e.mul(out, in_, val), r=r, w=w, cost=0.22 + fsz(out) / 1400.0)

    def dma(q, out, in_, r, w, **kw):
        nb = 128 * fsz(out) * (4 if in_.dtype == F32 else 2)
        if out.shape[0] < 128:
            nb = out.shape[0] * fsz(out) * 4
        return P.op(q, lambda e: e.dma_start(out=out, in_=in_, **kw), r=r, w=w, dma=True, nbytes=nb)

    def mset(eng, ap, val, w):
        return P.op(eng, lambda e: e.memset(ap, val), r=[], w=w, cost=vcost(eng, ap))

    def recip(out, in_, r, w):
        return P.op("dve", lambda e: e.reciprocal(out, in_), r=r, w=w, cost=vcost("dve", out))

    WP = 512

    def WK(name, k, c0):
        return f"{name}.{k}.{c0 // WP}"

    def load_w(A, W, wv, name, ncols, order=None):
        base = A.last_off
        npc = (ncols + WP - 1) // WP
        for k in range(8):
            for p_ in range(npc):
                c0 = p_ * WP
                c1 = min(ncols, c0 + WP)
                P.reg(f"{name}.{k}.{p_}", base + (k * ncols + c0) * 2, (c1 - c0) * 2)
        for p_ in (order if order is not None else range(npc)):
            c0 = p_ * WP
            c1 = min(ncols, c0 + WP)
            for k in range(8):
                dma("pool", W[:, k, c0:c1], wv[:, k, c0:c1], [], [f"{name}.{k}.{p_}"])

    cpi = [0]

    def cpany(out, in_, r, w):
        cpi[0] += 1
        return cp("act" if cpi[0] % 2 else "dve", out, in_, r, w)

    def sigmoid_chain(buf, src, rk, bk):
        act(buf, src, AF.Exp, rk, [bk], scale=-1.0)
        act(buf, buf, AF.Ln, [bk, "cst"], [bk], bias=cst[:, K_ONE:K_ONE + 1])
        act(buf, buf, AF.Exp, [bk], [bk], scale=-1.0)

    dma("sp", cst[:], consts_d, [], ["cst"])
    dma("sp", flags[:], flags_d, [], ["flags"])
    dma("sp", lnp[:], lnp_d, [], ["lnp"])
    dma("sp", convw[:], convw_d, [], ["convw"])
    dma("sp", gng[:], gng_d, [], ["gng"])
    cp("dve", cstb[:], cst[:], ["cst"], ["cstb"])

    def transpose_in(src, n_tok, dstv, dkey):
        A = Arena(P, nc, PBASE + 92160, LIMIT, "tin")
        xin = [A.alloc(f"xin{i}", [128, 2, D], F32) for i in range(2)]
        stg = [A.alloc(f"stg{i}", [128, 8, 256], F32) for i in range(2)]
        for ti in range(n_tok // 256):
            t0 = ti * 256
            sl = ti % 2
            dma("sp", xin[sl][:], src[t0:t0 + 256, :].rearrange("(s p) f -> p s f", p=128), [], [f"xin{sl}"])
            for k in range(8):
                b = k % 4
                for s in range(2):
                    tr(ps[b][:, s * 128:(s + 1) * 128], xin[sl][:, s, k * 128:(k + 1) * 128], ident,
                       [f"xin{sl}", "cst"], [f"ps{b}"])
                cpany(stg[sl][:, k, :], ps[b][:, 0:256], [f"ps{b}"], [f"stg{sl}"])
            dma("sp", dstv[:, :, t0:t0 + 256], stg[sl][:], [f"stg{sl}"], dkey(t0, 256))

    transpose_in(x_in, NT, Xv[0], lambda t0, n: xkeys(0, t0, n))
    transpose_in(mem_in, NSEG * NMEM, MEMT, lambda t0, n: [("MEMT", b) for b in range(t0 // 128, (t0 + n) // 128)])

    def ln_epilogue(xf, xfk, n, l, j, vb, sqb, lt, rstd, psm, psv, dstv, t0, dkeys):
        gcol = ((l * 4 + j) * 2 + 0) * 8
        bcol = ((l * 4 + j) * 2 + 1) * 8
        for d in range(8):
            cp("act", vb[:, d, :], xf[:, d, :], [f"{xfk}.{d}"], [f"vb.{d}"])
            mm(ps[psm][:, 0:n], onesm_b, vb[:, d, :], d == 0, d == 7, [f"vb.{d}", "cstb"], [f"ps{psm}"])
        tt("dve", xf[:], xf[:], ps[psm][:, 0:n].unsqueeze(1).to_broadcast([128, 8, n]), ALU.subtract,
           [xfk, f"ps{psm}"], [xfk])
        for d in range(8):
            act(sqb[d % 2][:], xf[:, d, :], AF.Square, [f"{xfk}.{d}"], [f"sqb{d % 2}"])
            mm(ps[psv][:, 0:n], onesm_b, sqb[d % 2][:], d == 0, d == 7, [f"sqb{d % 2}", "cstb"], [f"ps{psv}"])
        act(lt[:], ps[psv][:, 0:n], AF.Ln, [f"ps{psv}", "cst"], ["lt"], bias=cst[:, K_EPS_LN:K_EPS_LN + 1])
        act(rstd[:], lt[:], AF.Exp, ["lt"], ["rstd"], scale=-0.5)
        tt("dve", xf[:], xf[:], rstd[:].unsqueeze(1).to_broadcast([128, 8, n]), ALU.mult, [xfk, "rstd"], [xfk])
        for d in range(8):
            if d % 2 == 0:
                ts("pool", xf[:, d, :], xf[:, d, :], lnp[:, gcol + d:gcol + d + 1], lnp[:, bcol + d:bcol + d + 1],
                   ALU.mult, ALU.add, [f"{xfk}.{d}", "lnp"], [f"{xfk}.{d}"])
            else:
                act(xf[:, d, :], xf[:, d, :], AF.Identity, [f"{xfk}.{d}", "lnp"], [f"{xfk}.{d}"],
                    bias=lnp[:, bcol + d:bcol + d + 1], scale=lnp[:, gcol + d:gcol + d + 1])
        return dma("sp", dstv[:, :, t0:t0 + n], xf[:], [xfk], dkeys)

    def ffn_phase(l, j, wgu_d, wd_d, cur):
        A = Arena(P, nc, PBASE, LIMIT, f"f{l}{j}")
        Wgu = A.alloc("Wgu", [128, 8, 2 * DFF], BF16)
        load_w(A, Wgu, wgu_d[l].rearrange("(k p) f -> p k f", p=128), "Wgu", 2 * DFF, order=[0, 5, 6, 1, 7, 2, 8, 3, 9, 4, 10])
        Wd = A.alloc("Wd", [128, 22, D], BF16, split=22)
        n = TTF
        xb = [A.alloc(f"xb{i}", [128, 8, n], BF16) for i in range(2)]
        xf = A.alloc("xf", [128, 8, n], F32, split=8)
        h = A.alloc("h", [128, 22, n], BF16, split=22)
        vb = A.alloc("vb", [128, 8, n], BF16, split=8)
        sg = [A.alloc(f"sg{i}", [128, n], F32) for i in range(2)]
        gs = [A.alloc(f"gs{i}", [128, n], F32) for i in range(2)]
        sqb = [A.alloc(f"sqb{i}", [128, n], BF16) for i in range(2)]
        lt = A.alloc("lt", [128, n], F32)
        rstd = A.alloc("rstd", [128, n], F32)
        wdv = wd_d[l].rearrange("(f p) d -> p f d", p=128)
        for f in range(22):
            dma("pool", Wd[:, f, :], wdv[:, f, :], [], [f"Wd.{f}"], max_dma_last_dim=8192)
        ntile = NT // n

        def load(ti):
            t0 = ti * n
            dma("pool", xb[ti % 2][:], Xv[cur][:, :, t0:t0 + n], xkeys(cur, t0, n), [f"xb{ti % 2}"])

        load(0)
        for ti in range(ntile):
            t0 = ti * n
            sl = ti % 2
            if ti + 1 < ntile:
                load(ti + 1)
            dma("sp", xf[:], Xv[cur][:, :, t0:t0 + n], xkeys(cur, t0, n), ["xf"])
            for f in range(22):
                bg = (f % 2) * 2
                bu = bg + 1
                for k in range(8):
                    mm(ps[bg][:, 0:n], Wgu[:, k, f * 128:(f + 1) * 128], xb[sl][:, k, :], k == 0, k == 7,
                       [WK("Wgu", k, f * 128), f"xb{sl}"], [f"ps{bg}"])
                for k in range(8):
                    mm(ps[bu][:, 0:n], Wgu[:, k, DFF + f * 128:DFF + (f + 1) * 128], xb[sl][:, k, :], k == 0, k == 7,
                       [WK("Wgu", k, DFF + f * 128), f"xb{sl}"], [f"ps{bu}"])
                sigmoid_chain(sg[f % 2][:], ps[bg][:, 0:n], [f"ps{bg}"], f"sg{f % 2}")
                stt(gs[f % 2][:], sg[f % 2][:], 0.5, ps[bg][:, 0:n], ALU.mult, ALU.mult, [f"sg{f % 2}", f"ps{bg}"], [f"gs{f % 2}"])
                tt("dve", h[:, f, :], gs[f % 2][:], ps[bu][:, 0:n], ALU.mult, [f"gs{f % 2}", f"ps{bu}"], [f"h.{f}"])
            for d in range(8):
                b = 4 + d % 2
                for f in range(22):
                    mm(ps[b][:, 0:n], Wd[:, f, d * 128:(d + 1) * 128], h[:, f, :], f == 0, f == 21,
                       [f"Wd.{f}", f"h.{f}"], [f"ps{b}"])
                stt(xf[:, d, :], xf[:, d, :], ALPHA, ps[b][:, 0:n], ALU.mult, ALU.add, [f"ps{b}", f"xf.{d}"], [f"xf.{d}"])
            ln_epilogue(xf, "xf", n, l, j, vb, sqb, lt, rstd, 6, 7, Xv[1 - cur], t0, xkeys(1 - cur, t0, n))

    def gla_tile(bufs, kn, l, direction, first_in_scan, seg_start, link_flag, Win, xbh, XB, qk):
        (qz, kin, vz, lrs, zs, zh, zl, eq, ek, sml, kintok, attm, kvs, sps, w2b, gbias) = bufs
        fwd = direction == 0
        c_lr = C_LRF if fwd else C_LRB
        k_trid = K_TRID_F if fwd else K_TRID_B
        k_trix = K_TRIX_F if fwd else K_TRIX_B
        k_mask = K_MASK_F if fwd else K_MASK_B
        P.shift = -CHAIN_SHIFT if fwd else 0
        for k in range(8):
            mm(ps[4][0:16, :], Win[:, k, c_lr:c_lr + 16], xbh[:, k, 0:512], k == 0, k == 7, [WK("Win", k, c_lr), XB], ["ps4"])
        cp("act", lrs[0:16, :], ps[4][0:16, :], ["ps4"], [kn["lrs"]])
        mode, qraw, kraw, QR, KR = qk
        P.shift = 0
        for s in range(4):
            if mode == "load":
                break
            b = 7 if s % 2 == 0 else 4
            for k in range(8):
                mm(ps[b][:, :], xbh[:, k, s * 128:(s + 1) * 128], Win[:, k, C_V:C_V + 512], k == 0, k == 7,
                   [WK("Win", k, C_V), XB], [f"ps{b}"])
            cp("act", vz[0:64, s, 0, :], ps[b][0:64, :], [f"ps{b}"], [kn["vz"]])
            cp("dve", vz[64:128, s, 1, :], ps[b][64:128, :], [f"ps{b}"], [kn["vz"]])
        P.shift = -CHAIN_SHIFT if fwd else 0
        for s in range(4):
            b = 5 + s // 2
            mm(ps[b][:, (s % 2) * 256:(s % 2) * 256 + 256], lrs[0:16, s * 128:(s + 1) * 128],
               w2b[0:16, direction, :], True, True, [kn["lrs"], "w2b"], [f"ps{b}"])
        for hf in range(2):
            tt("dve", zs[:, 2 * hf:2 * hf + 2, :], ps[5 + hf][:, :].rearrange("p (s d) -> p s d", s=2),
               gbias[:, direction, :].unsqueeze(1).to_broadcast([128, 2, 256]), ALU.add,
               [f"ps{5 + hf}", "gbias"], [kn["zs"]])
        act(zs[:], zs[:], AF.Exp, [kn["zs"]], [kn["zs"]], scale=-1.0)
        act(zs[:], zs[:], AF.Ln, [kn["zs"], "cst"], [kn["zs"]], bias=cst[:, K_ONE:K_ONE + 1])
        cp("act", zh[:], zs[:], [kn["zs"]], [kn["zh"]])
        tt("dve", zl[:], zs[:], zh[:], ALU.subtract, [kn["zs"], kn["zh"]], [kn["zl"]])
        for dch in range(2):
            for s in range(4):
                mm(ps[5 + dch][:, s * 128:(s + 1) * 128], zh[:, s, dch * 128:(dch + 1) * 128],
                   cstb[:, k_trid:k_trid + 128], True, False, [kn["zh"], "cstb"], [f"ps{5 + dch}"])
                mm(ps[5 + dch][:, s * 128:(s + 1) * 128], zl[:, s, dch * 128:(dch + 1) * 128],
                   cstb[:, k_trid:k_trid + 128], False, True, [kn["zl"], "cstb"], [f"ps{5 + dch}"])
        for dch in range(2):
            for s in range(4):
                o4 = (dch * 4 + s) * 4
                mm(ps[4][:, o4:o4 + 4], zh[:, s, dch * 128:(dch + 1) * 128], cstb[:, k_trix:k_trix + 4],
                   True, False, [kn["zh"], "cstb"], ["ps4"])
                mm(ps[4][:, o4:o4 + 4], zl[:, s, dch * 128:(dch + 1) * 128], cstb[:, k_trix:k_trix + 4],
                   False, True, [kn["zl"], "cstb"], ["ps4"])
        for dch in range(2):
            act(eq[:, dch, :], ps[5 + dch][:, :], AF.Exp, [f"ps{5 + dch}"], [kn["eq"]])
            act(ek[:, dch, :], ps[5 + dch][:, :], AF.Exp, [f"ps{5 + dch}"], [kn["ek"]], scale=-1.0)
        xv4 = ps[4][:, 0:32].rearrange("p (a c) -> p a c", c=4)
        act(sml[:, 0, :].rearrange("p (a c) -> p a c", c=2), xv4[:, :, 0:2], AF.Exp, ["ps4"], [kn["sml"]])
        act(sml[:, 1, :].rearrange("p (a c) -> p a c", c=2), xv4[:, :, 2:4], AF.Exp, ["ps4"], [kn["sml"]])
        tt("dve", sml[:, 2, :], sml[:, 0, :], sml[:, 1, :], ALU.mult, [kn["sml"]], [kn["sml"]])
        for dch in range(2):
            bq = 5 + dch
            if mode == "save":
                for k in range(8):
                    mm(ps[bq][:, :], Win[:, k, C_Q + dch * 128:C_Q + (dch + 1) * 128], xbh[:, k, 0:512], k == 0, k == 7,
                       [WK("Win", k, C_Q + dch * 128), XB], [f"ps{bq}"])
                cp("act", qraw[:, dch, :], ps[bq][:, :], [f"ps{bq}"], [QR])
            for hh in range(2):
                pr = slice(hh * 64, hh * 64 + 64)
                stt(qz[pr, 2 * dch + hh, :], qraw[pr, dch, :], 0.125, eq[pr, dch, :], ALU.mult, ALU.mult,
                    [QR, kn["eq"]], [kn["qz"]])
        for dch in range(2):
            bk = 7 if dch == 0 else 4
            if mode == "save":
                for k in range(8):
                    mm(ps[bk][:, :], Win[:, k, C_K + dch * 128:C_K + (dch + 1) * 128], xbh[:, k, 0:512], k == 0, k == 7,
                       [WK("Win", k, C_K + dch * 128), XB], [f"ps{bk}"])
                cp("act", kraw[:, dch, :], ps[bk][:, :], [f"ps{bk}"], [KR])
            tt("dve", kin[:, dch, :], kraw[:, dch, :], ek[:, dch, :], ALU.mult, [KR, kn["ek"]], [kn["kin"]])
        P.shift = 0
        pst = ps[5][:, :].bitcast(BF16)
        for s in range(4):
            for dch in range(2):
                o = (s * 2 + dch) * 128
                tr(pst[:, o:o + 128], kin[:, dch, s * 128:(s + 1) * 128], identb, [kn["kin"], "cstb"], ["ps5"])
        cp("act", kintok[:].rearrange("p s d -> p (s d)"), pst[:, :], ["ps5"], [kn["kintok"]])
        for s in range(4):
            b = 6 + s % 2
            for hd in range(4):
                mm(ps[b][:, hd * 128:(hd + 1) * 128], kin[:, hd // 2, s * 128:(s + 1) * 128],
                   qz[:, hd, s * 128:(s + 1) * 128], True, True, [kn["kin"], kn["qz"]], [f"ps{b}"])
            tt("dve", attm[:, s, :, :], ps[b][:, :].rearrange("p (h i) -> p h i", h=4),
               cst[:, k_mask:k_mask + 128].unsqueeze(1).to_broadcast([128, 4, 128]), ALU.mult,
               [f"ps{b}", "cst"], [kn["attm"]])
        for c in range(8):
            s = c // 2
            par = c % 2
            b = 4 if c % 2 == 0 else 5
            for pair in range(2):
                for hh in range(2):
                    hd = 2 * pair + hh
                    mm(ps[b][hh * 64:hh * 64 + 64, pair * 128:(pair + 1) * 128],
                       kintok[:, s, hd * 64:(hd + 1) * 64], vz[:, s, par, hd * 128:(hd + 1) * 128],
                       True, True, [kn["kintok"], kn["vz"]], [f"ps{b}"])
            for pair in range(2):
                col = pair * 8 + c
                ts("dve", kvs[:, c, pair, :], ps[b][:, pair * 128:(pair + 1) * 128], sml[:, 1, col:col + 1], None,
                   ALU.mult, None, [f"ps{b}", kn["sml"]], [kn["kvs"] + f".{c * 2 + pair}"])
        if seg_start:
            if first_in_scan or link_flag is None:
                mset("dve", Sst[:], 0.0, ["Sst"])
            else:
                ts("dve", Sst[:], Sst[:], flags[:, link_flag:link_flag + 1], None, ALU.mult, None,
                   ["Sst", "flags"], ["Sst"])
        order = list(range(8)) if fwd else list(range(7, -1, -1))
        prev_c = None
        for c in order:
            for pair in range(2):
                col = pair * 8 + c
                if prev_c is None:
                    src = Sst[:, pair, :]
                    srck = f"Sst.{pair}"
                else:
                    src = kvs[:, prev_c, pair, :]
                    srck = kn["kvs"] + f".{prev_c * 2 + pair}"
                dstk = kn["kvs"] + f".{c * 2 + pair}"
                act(sps[:, c, pair, :], src, AF.Copy, [srck, kn["sml"]], [kn["sps"] + f".{c}"],
                    scale=sml[:, 0, col:col + 1])
                stt(kvs[:, c, pair, :], src, sml[:, 2, col:col + 1], kvs[:, c, pair, :], ALU.mult, ALU.add,
                    [srck, kn["sml"], dstk], [dstk])
            prev_c = c
        for pair in range(2):
            cp("dve", Sst[:, pair, :], kvs[:, prev_c, pair, :], [kn["kvs"] + f".{prev_c * 2 + pair}"], [f"Sst.{pair}"])
        obanks = [0, 1, 2, 3]
        for hd in range(4):
            b = obanks[hd]
            for s in range(4):
                mm(ps[b][:, s * 128:(s + 1) * 128], vz[:, s, 0, hd * 128:(hd + 1) * 128],
                   attm[:, s, hd, :], True, False, [kn["vz"], kn["attm"]], [f"ps{b}"])
                mm(ps[b][:, s * 128:(s + 1) * 128], vz[:, s, 1, hd * 128:(hd + 1) * 128],
                   attm[:, s, hd, :], False, False, [kn["vz"], kn["attm"]], [f"ps{b}"])
                for par in range(2):
                    c = 2 * s + par
                    mm(ps[b][:, c * 64:(c + 1) * 64], sps[:, c, hd // 2, :], qz[:, hd, c * 64:(c + 1) * 64],
                       False, par == 1, [kn["sps"] + f".{c}", kn["qz"]], [f"ps{b}"])
        return obanks

    GLA_SPECS = [("qz", [128, 4, 512], BF16, None), ("kin", [128, 2, 512], BF16, None),
                 ("vz", [128, 4, 2, 512], BF16, None), ("lrs", [128, 512], BF16, None),
                 ("zs", [128, 4, 256], F32, None), ("zh", [128, 4, 256], BF16, None), ("zl", [128, 4, 256], BF16, None),
                 ("eq", [128, 2, 512], F32, None), ("ek", [128, 2, 512], F32, None), ("sml", [128, 3, 16], F32, None),
                 ("kintok", [128, 4, 256], BF16, None), ("attm", [128, 4, 4, 128], BF16, None),
                 ("kvs", [128, 8, 2, 128], F32, 16), ("sps", [128, 8, 2, 128], BF16, 8)]

    shared = {}

    def gla_alloc(A, dbl, reuse=False):
        w2b = A.alloc("w2b", [128, 2, 256], BF16, reuse=shared["w2b"] if reuse else None)
        if not reuse:
            shared["w2b"] = (w2b, A.last_off)
        gbias = A.alloc("gbias", [128, 2, 256], F32, reuse=shared["gbias"] if reuse else None)
        if not reuse:
            shared["gbias"] = (gbias, A.last_off)
        slots = []
        kns = []
        single = {}
        for sl in range(2):
            bl = []
            kn = {}
            for (nm, shp, dt, sp_) in GLA_SPECS:
                if nm in dbl:
                    key = f"{nm}{sl}"
                    bl.append(A.alloc(key, shp, dt, split=sp_))
                else:
                    key = nm
                    if nm not in single:
                        single[nm] = A.alloc(key, shp, dt, split=sp_)
                    bl.append(single[nm])
                kn[nm] = key
            slots.append(tuple(bl) + (w2b, gbias))
            kns.append(kn)
        return slots, kns

    def gla_init(slots, kns, l, load=True):
        done = set()
        for sl in range(2):
            for nm, idx in (("qz", 0), ("vz", 2)):
                key = kns[sl][nm]
                if key in done:
                    continue
                done.add(key)
                mset("pool", slots[sl][idx][:], 0.0, [key])
        if not load:
            return
        w2b, gbias = slots[0][-2], slots[0][-1]
        dma("pool", w2b[0:16, :, :], w_g2[l].rearrange("d r c -> r d c"), [], ["w2b"])
        dma("sp", gbias[:], gbias_d[:, l * 512:(l + 1) * 512].rearrange("p (d c) -> p d c", d=2), [], ["gbias"])

    NTILE = NT // TT
    TPS = SEGLEN // TT
    CHAIN_SHIFT = 700

    def load_xbh(xbh, XB, cur, t0, halo):
        dma("pool", xbh[:, :, 0:512], Xv[cur][:, :, t0:t0 + 512], xkeys(cur, t0, 512), [XB])
        if halo:
            tl = max(t0 - 1, 0)
            trr = min(t0 + 512, NT - 1)
            dma("pool", xbh[:, :, 512:513], Xv[cur][:, :, tl:tl + 1], xkeys(cur, tl, 1), [XB], allow_slow_non_contiguous=True)
            dma("pool", xbh[:, :, 513:514], Xv[cur][:, :, trr:trr + 1], xkeys(cur, trr, 1), [XB], allow_slow_non_contiguous=True)

    def mixer_pass_b(l, cur):
        A = Arena(P, nc, PBASE, LIMIT, f"mb{l}")
        Win = A.alloc("Win", [128, 8, 3104], BF16)
        shared["Win"] = (Win, A.last_off)
        load_w(A, Win, w_in[l].rearrange("(k p) f -> p k f", p=128), "Win", 3104, order=[6, 4, 3, 5, 0, 1, 2])
        xbhs = [A.alloc(f"xbh{i}", [128, 8, 514], BF16) for i in range(2)]
        slots, kns = gla_alloc(A, {"qz", "kin", "vz", "lrs", "zs", "zh", "zl", "eq", "ek", "sml", "kintok", "attm", "kvs", "sps"})
        obs = [A.alloc(f"obs{i}", [128, 4, 512], F32) for i in range(2)]
        qraws = [A.alloc(f"qraw{i}", [128, 2, 512], BF16) for i in range(2)]
        kraws = [A.alloc(f"kraw{i}", [128, 2, 512], BF16) for i in range(2)]
        gla_init(slots, kns, l)
        first = True
        tiles = list(range(NTILE - 1, -1, -1))
        load_xbh(xbhs[0], "xbh0", cur, tiles[0] * TT, False)
        for n_, ti in enumerate(tiles):
            t0 = ti * TT
            seg = ti // TPS
            seg_start = (ti % TPS) == TPS - 1
            link = 1 if (seg == 0 and NSEG > 1) else None
            sl = n_ % 2
            if n_ + 1 < len(tiles):
                load_xbh(xbhs[1 - sl], f"xbh{1 - sl}", cur, tiles[n_ + 1] * TT, False)
            obanks = gla_tile(slots[sl], kns[sl], l, 1, first, seg_start, link, Win, xbhs[sl], f"xbh{sl}",
                              ("save", qraws[sl], kraws[sl], f"qraw{sl}", f"kraw{sl}"))
            first = False
            dma("sp", QSs[:, :, t0:t0 + 512], qraws[sl][:], [f"qraw{sl}"], [("QS", ti)])
            dma("sp", KSs[:, :, t0:t0 + 512], kraws[sl][:], [f"kraw{sl}"], [("KS", ti)])
            vzb = slots[sl][2]
            vsv = VSs[t0:t0 + 512, :].rearrange("(s p) c -> p s c", p=128)
            dma("sp", vsv[0:64], vzb[0:64, :, 0, :], [kns[sl]["vz"]], [("VS", ti, 0)])
            dma("sp", vsv[64:128], vzb[64:128, :, 1, :], [kns[sl]["vz"]], [("VS", ti, 1)])
            for hd in range(4):
                cpany(obs[sl][:, hd, :], ps[obanks[hd]][:, :], [f"ps{obanks[hd]}"], [f"obs{sl}"])
            dma("sp", OB[:, :, t0:t0 + 512], obs[sl][:], [f"obs{sl}"], [("OB", ti)])

    def mixer_pass_f(l, cur):
        A = Arena(P, nc, PBASE, LIMIT, f"mf{l}")
        Win = A.alloc("Win", [128, 8, 3104], BF16, reuse=shared["Win"])
        xbhs = [A.alloc(f"xbh{i}", [128, 8, 514], BF16) for i in range(2)]
        slots, kns = gla_alloc(A, {"qz", "kin", "vz", "sml", "kintok", "attm", "sps"}, reuse=True)
        obl = A.alloc("obl", [128, 4, 512], F32)
        oh = [A.alloc(f"oh{i}", [128, 512], F32) for i in range(2)]
        sqh = [A.alloc(f"sqh{i}", [128, 512], BF16) for i in range(2)]
        rsh = [A.alloc(f"rsh{i}", [128, 512], F32) for i in range(2)]
        sgh = [A.alloc(f"sgh{i}", [128, 512], F32) for i in range(2)]
        ons_ = A.alloc("ons0", [128, 4, 512], BF16)
        ons = [ons_, ons_]
        gcs = [A.alloc(f"gcs{i}", [128, 514], F32) for i in range(2)]
        ux = [A.alloc(f"ux{i}", [128, 514], F32) for i in range(2)]
        cac = [A.alloc(f"cac{i}", [128, 512], F32) for i in range(2)]
        cgs_ = A.alloc("cgs0", [128, 4, 512], BF16)
        cgs = [cgs_, cgs_]
        qraws = [A.alloc(f"qraw{i}", [128, 2, 512], BF16) for i in range(2)]
        kraws = [A.alloc(f"kraw{i}", [128, 2, 512], BF16) for i in range(2)]
        gla_init(slots, kns, l, load=False)

        def load_qkv(ti):
            s2 = ti % 2
            t0_ = ti * TT
            dma("sp", qraws[s2][:], QSs[:, :, t0_:t0_ + 512], [("QS", ti)], [f"qraw{s2}"])
            dma("sp", kraws[s2][:], KSs[:, :, t0_:t0_ + 512], [("KS", ti)], [f"kraw{s2}"])
            vzb = slots[s2][2]
            vsv = VSs[t0_:t0_ + 512, :].rearrange("(s p) c -> p s c", p=128)
            dma("sp", vzb[0:64, :, 0, :], vsv[0:64], [("VS", ti, 0)], [kns[s2]["vz"]])
            dma("sp", vzb[64:128, :, 1, :], vsv[64:128], [("VS", ti, 1)], [kns[s2]["vz"]])

        load_qkv(0)
        load_xbh(xbhs[0], "xbh0", cur, 0, True)
        for ti in range(NTILE):
            t0 = ti * TT
            seg = ti // TPS
            seg_start = (ti % TPS) == 0
            seg_end = (ti % TPS) == TPS - 1
            link = 1 if (seg == 1) else None
            sl = ti % 2
            xbh = xbhs[sl]
            XB = f"xbh{sl}"
            if ti + 1 < NTILE:
                load_xbh(xbhs[1 - sl], f"xbh{1 - sl}", cur, (ti + 1) * TT, True)
                load_qkv(ti + 1)
            dma("sp", obl[:], OB[:, :, t0:t0 + 512], [("OB", ti)], ["obl"])
            obanks = gla_tile(slots[sl], kns[sl], l, 0, ti == 0, seg_start, link, Win, xbh, XB,
                              ("load", qraws[sl], kraws[sl], f"qraw{sl}", f"kraw{sl}"))
            for hd in range(4):
                i2 = hd % 2
                b = obanks[hd]
                tt("dve", oh[i2][:], ps[b][:, :], obl[:, hd, :], ALU.add, [f"ps{b}", "obl"], [f"oh{i2}"])
                act(sqh[i2][:], oh[i2][:], AF.Square, [f"oh{i2}"], [f"sqh{i2}"])
                mm(ps[6][:, :], onesr_b, sqh[i2][:], True, True, [f"sqh{i2}", "cstb"], ["ps6"])
                act(rsh[i2][:], ps[6][:, :], AF.Ln, ["ps6", "cst"], [f"rsh{i2}"], bias=cst[:, K_EPS_RMS:K_EPS_RMS + 1])
                act(rsh[i2][:], rsh[i2][:], AF.Exp, [f"rsh{i2}"], [f"rsh{i2}"], scale=-0.5)
                bgp = 5 if i2 == 0 else 7
                for k in range(8):
                    mm(ps[bgp][:, :], Win[:, k, C_G + hd * 128:C_G + (hd + 1) * 128], xbh[:, k, 0:512], k == 0, k == 7,
                       [WK("Win", k, C_G + hd * 128), XB], [f"ps{bgp}"])
                sigmoid_chain(sgh[i2][:], ps[bgp][:, :], [f"ps{bgp}"], f"sgh{i2}")
                tt("dve", sgh[i2][:], sgh[i2][:], ps[bgp][:, :], ALU.mult, [f"sgh{i2}", f"ps{bgp}"], [f"sgh{i2}"])
                stt(oh[i2][:], oh[i2][:], gng[:, l:l + 1], rsh[i2][:], ALU.mult, ALU.mult,
                    [f"oh{i2}", "gng", f"rsh{i2}"], [f"oh{i2}"])
                tt("dve", ons[sl][:, hd, :], oh[i2][:], sgh[i2][:], ALU.mult, [f"oh{i2}", f"sgh{i2}"], ["ons0"])
            dma("sp", ONs[:, :, t0:t0 + 512], ons[sl][:], ["ons0"], [("ON", ti)])
            fl_l = None if seg_start and seg != 1 else (1 if seg_start else "one")
            fl_r = None if seg_end and seg != 0 else (1 if seg_end else "one")
            if seg_end and seg == 0 and NSEG == 1:
                fl_r = None
            for ch in range(4):
                i2 = ch % 2
                bgc = 0 if i2 == 0 else 2
                bh = 1 if i2 == 0 else 3
                for k in range(8):
                    mm(ps[bgc][:, :], Win[:, k, C_GC + ch * 128:C_GC + (ch + 1) * 128], xbh[:, k, 0:512], k == 0, k == 7,
                       [WK("Win", k, C_GC + ch * 128), XB], [f"ps{bgc}"])
                for k in range(8):
                    mm(ps[7][:, 0:2], Win[:, k, C_GC + ch * 128:C_GC + (ch + 1) * 128], xbh[:, k, 512:514], k == 0, k == 7,
                       [WK("Win", k, C_GC + ch * 128), XB], ["ps7"])
                cp("act", gcs[i2][:, 1:513], ps[bgc][:, :], [f"ps{bgc}"], [f"gcs{i2}"])
                cp("act", gcs[i2][:, 0:1], ps[7][:, 0:1], ["ps7"], [f"gcs{i2}"])
                cp("act", gcs[i2][:, 513:514], ps[7][:, 1:2], ["ps7"], [f"gcs{i2}"])
                for k in range(8):
                    mm(ps[bh][:, :], Win[:, k, C_H + ch * 128:C_H + (ch + 1) * 128], xbh[:, k, 0:512], k == 0, k == 7,
                       [WK("Win", k, C_H + ch * 128), XB], [f"ps{bh}"])
                for k in range(8):
                    mm(ps[7][:, 2:4], Win[:, k, C_H + ch * 128:C_H + (ch + 1) * 128], xbh[:, k, 512:514], k == 0, k == 7,
                       [WK("Win", k, C_H + ch * 128), XB], ["ps7"])
                tt("dve", ux[i2][:, 1:513], ps[bh][:, :], gcs[i2][:, 1:513], ALU.mult, [f"ps{bh}", f"gcs{i2}"], [f"ux{i2}"])
                for (col, pcol, fl) in ((0, 2, fl_l), (513, 3, fl_r)):
                    if fl is None:
                        mset("dve", ux[i2][:, col:col + 1], 0.0, [f"ux{i2}"])
                    elif fl == "one":
                        tt("dve", ux[i2][:, col:col + 1], ps[7][:, pcol:pcol + 1], gcs[i2][:, col:col + 1], ALU.mult,
                           ["ps7", f"gcs{i2}"], [f"ux{i2}"])
                    else:
                        stt(ux[i2][:, col:col + 1], ps[7][:, pcol:pcol + 1], flags[:, 1:2], gcs[i2][:, col:col + 1],
                            ALU.mult, ALU.mult, ["ps7", f"gcs{i2}", "flags"], [f"ux{i2}"])
                w0 = convw[:, (l * 3 + 0) * 4 + ch:(l * 3 + 0) * 4 + ch + 1]
                w1 = convw[:, (l * 3 + 1) * 4 + ch:(l * 3 + 1) * 4 + ch + 1]
                w2 = convw[:, (l * 3 + 2) * 4 + ch:(l * 3 + 2) * 4 + ch + 1]
                act(cac[i2][:], ux[i2][:, 1:513], AF.Copy, [f"ux{i2}", "convw"], [f"cac{i2}"], scale=w1)
                stt(cac[i2][:], ux[i2][:, 0:512], w0, cac[i2][:], ALU.mult, ALU.add, [f"ux{i2}", "convw", f"cac{i2}"], [f"cac{i2}"])
                stt(cac[i2][:], ux[i2][:, 2:514], w2, cac[i2][:], ALU.mult, ALU.add, [f"ux{i2}", "convw", f"cac{i2}"], [f"cac{i2}"])
                for k in range(8):
                    mm(ps[bgc][:, :], Win[:, k, C_GB + ch * 128:C_GB + (ch + 1) * 128], xbh[:, k, 0:512], k == 0, k == 7,
                       [WK("Win", k, C_GB + ch * 128), XB], [f"ps{bgc}"])
                tt("dve", cgs[sl][:, ch, :], cac[i2][:], ps[bgc][:, :], ALU.mult, [f"cac{i2}", f"ps{bgc}"], ["cgs0"])
            dma("sp", CGs[:, :, t0:t0 + 512], cgs[sl][:], ["cgs0"], [("CG", ti)])

    def mixer_pass_g(l, cur):
        A = Arena(P, nc, PBASE, LIMIT, f"mg{l}")
        Wm = A.alloc("Wm", [128, 8, 2048], BF16)
        load_w(A, Wm, w_in[l].rearrange("(k p) f -> p k f", p=128)[:, :, C_MC:DIN], "Wm", 2048, order=[0, 2, 1, 3])
        Wco = A.alloc("Wco", [128, 4, D], BF16)
        Wgo = A.alloc("Wgo", [128, 4, D], BF16)
        Wmo = A.alloc("Wmo", [128, 8, D], BF16)
        wmo_off = A.last_off
        xb = [A.alloc(f"xb{i}", [128, 8, 512], BF16) for i in range(2)]
        xfs = [A.alloc(f"xf{i}", [128, 8, 512], F32, split=8) for i in range(2)]
        vb = A.alloc("vb", [128, 8, 512], BF16, split=8)
        onl = A.alloc("onl", [128, 4, 512], BF16)
        cgl = A.alloc("cgl", [128, 4, 512], BF16)
        sg = [A.alloc(f"sg{i}", [128, 512], F32) for i in range(2)]
        sc = [A.alloc(f"sc{i}", [128, 512], F32) for i in range(2)]
        t1 = A.alloc("t1", [128, 512], F32)
        t2 = A.alloc("t2", [128, 512], F32)
        mrg = A.alloc("mrg", [128, 8, 512], BF16, split=8)
        sqb = [A.alloc(f"sqb{i}", [128, 512], BF16) for i in range(2)]
        lt = A.alloc("lt", [128, 512], F32)
        rstd = A.alloc("rstd", [128, 512], F32)
        dma("pool", Wco[:], w_co[l].rearrange("(f p) d -> p f d", p=128), [], ["Wco"], max_dma_last_dim=8192)
        dma("pool", Wgo[:], w_go[l].rearrange("(f p) d -> p f d", p=128), [], ["Wgo"], max_dma_last_dim=8192)
        A.last_off = wmo_off
        load_w(A, Wmo, w_mo[l].rearrange("(k p) d -> p k d", p=128), "Wmo", D)

        def load(ti):
            t0 = ti * TT
            dma("pool", xb[ti % 2][:], Xv[cur][:, :, t0:t0 + 512], xkeys(cur, t0, 512), [f"xb{ti % 2}"])

        load(0)
        for ti in range(NTILE):
            t0 = ti * TT
            sl = ti % 2
            if ti + 1 < NTILE:
                load(ti + 1)
            dma("sp", onl[:], ONs[:, :, t0:t0 + 512], [("ON", ti)], ["onl"])
            dma("sp", cgl[:], CGs[:, :, t0:t0 + 512], [("CG", ti)], ["cgl"])
            xf = xfs[sl]
            XF = f"xf{sl}"
            dma("sp", xf[:], Xv[cur][:, :, t0:t0 + 512], xkeys(cur, t0, 512), [XF])
            for d in range(8):
                i2 = d % 2
                for k in range(8):
                    mm(ps[0][:, :], Wm[:, k, d * 128:(d + 1) * 128], xb[sl][:, k, :], k == 0, k == 7,
                       [WK("Wm", k, d * 128), f"xb{sl}"], ["ps0"])
                sigmoid_chain(sc[i2][:], ps[0][:, :], ["ps0"], f"sc{i2}")
                for k in range(8):
                    mm(ps[1][:, :], Wm[:, k, 1024 + d * 128:1024 + (d + 1) * 128], xb[sl][:, k, :], k == 0, k == 7,
                       [WK("Wm", k, 1024 + d * 128), f"xb{sl}"], ["ps1"])
                sigmoid_chain(sg[i2][:], ps[1][:, :], ["ps1"], f"sg{i2}")
                for f in range(4):
                    mm(ps[2][:, :], Wco[:, f, d * 128:(d + 1) * 128], cgl[:, f, :], f == 0, f == 3, ["Wco", "cgl"], ["ps2"])
                for f in range(4):
                    mm(ps[3][:, :], Wgo[:, f, d * 128:(d + 1) * 128], onl[:, f, :], f == 0, f == 3, ["Wgo", "onl"], ["ps3"])
                tt("dve", t1[:], sc[i2][:], ps[2][:, :], ALU.mult, [f"sc{i2}", "ps2"], ["t1"])
                tt("dve", t2[:], sg[i2][:], ps[3][:, :], ALU.mult, [f"sg{i2}", "ps3"], ["t2"])
                tt("dve", mrg[:, d, :], t1[:], t2[:], ALU.add, ["t1", "t2"], [f"mrg.{d}"])
            for d in range(8):
                b = 4 + d % 2
                for k in range(8):
                    mm(ps[b][:, :], Wmo[:, k, d * 128:(d + 1) * 128], mrg[:, k, :], k == 0, k == 7,
                       [WK("Wmo", k, d * 128), f"mrg.{k}"], [f"ps{b}"])
                stt(xf[:, d, :], xf[:, d, :], ALPHA, ps[b][:, :], ALU.mult, ALU.add, [f"ps{b}", f"{XF}.{d}"], [f"{XF}.{d}"])
            ln_epilogue(xf, XF, 512, l, 1, vb, sqb, lt, rstd, 6, 7, Xv[1 - cur], t0, xkeys(1 - cur, t0, 512))

    def xattn_phase(l, cur):
        A = Arena(P, nc, PBASE, LIMIT, f"xa{l}")
        Wq = A.alloc("Wq", [128, 8, D], BF16)
        wq_off = A.last_off
        Wkv = A.alloc("Wkv", [128, 8, 2 * D], BF16)
        load_w(A, Wkv, w_xkv[l].rearrange("(k p) d -> p k d", p=128), "Wkv", 2 * D)
        A.last_off = wq_off
        load_w(A, Wq, w_xq[l].rearrange("(k p) d -> p k d", p=128), "Wq", D)
        Wo = A.alloc("Wo", [128, 8, D], BF16)
        load_w(A, Wo, w_xo[l].rearrange("(k p) d -> p k d", p=128), "Wo", D)
        memT = A.alloc("memT", [128, 8, NMEM], BF16)
        KT = [A.alloc(f"KT{s}", [128, 8, NMEM], BF16) for s in range(NSEG)]
        Vt = [A.alloc(f"Vt{s}", [128, 2, D], BF16) for s in range(NSEG)]
        xb = [A.alloc(f"xb{i}", [128, 8, 512], BF16) for i in range(2)]
        xfs = [A.alloc(f"xf{i}", [128, 8, 512], F32, split=8) for i in range(2)]
        vb = A.alloc("vb", [128, 8, 512], BF16, split=8)
        qs = A.alloc("qs", [128, 8, 512], BF16, split=8)
        pT = [A.alloc(f"pT{i}", [128, 2, 512], BF16) for i in range(2)]
        rden = [A.alloc(f"rden{i}", [128, 512], F32) for i in range(2)]
        oa = A.alloc("oa", [128, 8, 512], BF16, split=8)
        sqb = [A.alloc(f"sqb{i}", [128, 512], BF16) for i in range(2)]
        lt = A.alloc("lt", [128, 512], F32)
        rstd = A.alloc("rstd", [128, 512], F32)
        for sgm in range(NSEG):
            dma("pool", memT[:], MEMT[:, :, sgm * NMEM:(sgm + 1) * NMEM], [("MEMT", 2 * sgm), ("MEMT", 2 * sgm + 1)], ["memT"])
            for dch in range(8):
                b = dch % 2
                for k in range(8):
                    mm(ps[b][:, 0:NMEM], Wkv[:, k, dch * 128:(dch + 1) * 128], memT[:, k, :], k == 0, k == 7,
                       [WK("Wkv", k, dch * 128), "memT"], [f"ps{b}"])
                cpany(KT[sgm][:, dch, :], ps[b][:, 0:NMEM], [f"ps{b}"], [f"KT{sgm}"])
            for mch in range(2):
                for hf in range(2):
                    b = 2 + hf
                    for k in range(8):
                        mm(ps[b][:, :], memT[:, k, mch * 128:(mch + 1) * 128], Wkv[:, k, D + hf * 512:D + (hf + 1) * 512],
                           k == 0, k == 7, [WK("Wkv", k, D + hf * 512), "memT"], [f"ps{b}"])
                    cpany(Vt[sgm][:, mch, hf * 512:(hf + 1) * 512], ps[b][:, :], [f"ps{b}"], [f"Vt{sgm}"])

        def load(ti):
            t0 = ti * TT
            dma("pool", xb[ti % 2][:], Xv[cur][:, :, t0:t0 + 512], xkeys(cur, t0, 512), [f"xb{ti % 2}"])

        load(0)
        for ti in range(NTILE):
            t0 = ti * TT
            sl = ti % 2
            sgm = ti // TPS
            if ti + 1 < NTILE:
                load(ti + 1)
            xf = xfs[sl]
            XF = f"xf{sl}"
            dma("sp", xf[:], Xv[cur][:, :, t0:t0 + 512], xkeys(cur, t0, 512), [XF])
            for dch in range(8):
                b = dch % 2
                for k in range(8):
                    mm(ps[b][:, :], Wq[:, k, dch * 128:(dch + 1) * 128], xb[sl][:, k, :], k == 0, k == 7,
                       [WK("Wq", k, dch * 128), f"xb{sl}"], [f"ps{b}"])
                cpany(qs[:, dch, :], ps[b][:, :], [f"ps{b}"], [f"qs.{dch}"])
            for hd in range(4):
                i2 = hd % 2
                for mch in range(2):
                    b = 2 + mch
                    for dd in range(2):
                        dch = 2 * hd + dd
                        mm(ps[b][:, :], KT[sgm][:, dch, mch * 128:(mch + 1) * 128], qs[:, dch, :], dd == 0, dd == 1,
                           [f"KT{sgm}", f"qs.{dch}"], [f"ps{b}"])
                    act(pT[i2][:, mch, :], ps[b][:, :], AF.Exp, [f"ps{b}"], [f"pT{i2}"], scale=1.0 / 16.0)
                for mch in range(2):
                    mm(ps[4][:, :], onesb, pT[i2][:, mch, :], mch == 0, mch == 1, ["cstb", f"pT{i2}"], ["ps4"])
                act(rden[i2][:], ps[4][:, :], AF.Ln, ["ps4"], [f"rden{i2}"])
                act(rden[i2][:], rden[i2][:], AF.Exp, [f"rden{i2}"], [f"rden{i2}"], scale=-1.0)
                for dvc in range(2):
                    b = 5 + dvc
                    for mch in range(2):
                        mm(ps[b][:, :], Vt[sgm][:, mch, hd * 256 + dvc * 128:hd * 256 + (dvc + 1) * 128], pT[i2][:, mch, :],
                           mch == 0, mch == 1, [f"Vt{sgm}", f"pT{i2}"], [f"ps{b}"])
                    tt("dve", oa[:, 2 * hd + dvc, :], ps[b][:, :], rden[i2][:], ALU.mult, [f"ps{b}", f"rden{i2}"],
                       [f"oa.{2 * hd + dvc}"])
            for d in range(8):
                b = d % 2
                for k in range(8):
                    mm(ps[b][:, :], Wo[:, k, d * 128:(d + 1) * 128], oa[:, k, :], k == 0, k == 7,
                       [WK("Wo", k, d * 128), f"oa.{k}"], [f"ps{b}"])
                stt(xf[:, d, :], xf[:, d, :], ALPHA, ps[b][:, :], ALU.mult, ALU.add, [f"ps{b}", f"{XF}.{d}"], [f"{XF}.{d}"])
            ln_epilogue(xf, XF, 512, l, 2, vb, sqb, lt, rstd, 6, 7, Xv[1 - cur], t0, xkeys(1 - cur, t0, 512))

    cur = 0
    P.marks = [("pro", 0)]
    for l in range(L):
        P.marks.append((f"ffn1.{l}", len(P.ops)))
        ffn_phase(l, 0, w_gu1, w_d1, cur); cur = 1 - cur
        P.marks.append((f"passB.{l}", len(P.ops)))
        mixer_pass_b(l, cur)
        P.marks.append((f"passF.{l}", len(P.ops)))
        mixer_pass_f(l, cur)
        P.marks.append((f"passG.{l}", len(P.ops)))
        mixer_pass_g(l, cur); cur = 1 - cur
        P.marks.append((f"xattn.{l}", len(P.ops)))
        xattn_phase(l, cur); cur = 1 - cur
        P.marks.append((f"ffn2.{l}", len(P.ops)))
        ffn_phase(l, 3, w_gu2, w_d2, cur); cur = 1 - cur
    P.marks.append(("epi", len(P.ops)))

    A = Arena(P, nc, PBASE, LIMIT, "tout")
    xt = [A.alloc(f"xt{i}", [128, 8, 256], F32) for i in range(2)]
    ys = [A.alloc(f"ys{i}", [128, 2, D], F32) for i in range(2)]
    outs = []
    for ti in range(NT // 256):
        t0 = ti * 256
        sl = ti % 2
        dma("sp", xt[sl][:], Xv[cur][:, :, t0:t0 + 256], xkeys(cur, t0, 256), [f"xt{sl}"])
        for s in range(2):
            for k in range(8):
                b = (s * 2 + k // 4)
                tr(ps[b][:, (k % 4) * 128:(k % 4 + 1) * 128], xt[sl][:, k, s * 128:(s + 1) * 128], ident,
                   [f"xt{sl}", "cst"], [f"ps{b}"])
            for hf in range(2):
                b = s * 2 + hf
                cpany(ys[sl][:, s, hf * 512:(hf + 1) * 512], ps[b][:, :], [f"ps{b}"], [f"ys{sl}"])
        outs.append(dma("sp", y_out[t0:t0 + 256, :].rearrange("(s p) f -> p s f", p=128), ys[sl][:], [f"ys{sl}"],
                        [("Y", ti)]))
    P.finalize(outs)

    semnames = ["pe", "act", "dve", "pool"] + [(q, i) for q in ("sp", "pool", "act") for i in range(NDMASEM)]
    from contextlib import ExitStack
    with ExitStack() as st:
        sems = {}
        for k in semnames:
            nm = k if isinstance(k, str) else f"d{k[0]}{k[1]}"
            sems[k] = st.enter_context(nc.semaphore("s_" + nm))
        block = st.enter_context(nc.Block())
        P.emit(block, sems)
    nc._prog_stats = (len(P.ops), P.nwaits)
    nc._sim = (P.makespan, P.busy)
    nc._P = P
    return nc


def prep_shared(inputs, L=4):
    sh = {}
    for k in ("ffn1_w_gu", "ffn1_w_down", "ffn2_w_gu", "ffn2_w_down", "w_mix_in", "w_conv_out", "w_gla_out",
              "w_mix_out", "gla_gate_w2", "xa_w_q", "xa_w_kv", "xa_w_o"):
        sh[k] = np.ascontiguousarray(np.asarray(inputs[k], dtype=np.float32))
    sh["consts"] = make_consts()
    ln_g = np.asarray(inputs["ln_g"], np.float32); ln_b = np.asarray(inputs["ln_b"], np.float32)
    lnp = np.zeros((128, L * 64), np.float32)
    for l in range(L):
        for j in range(4):
            for gb, arr in enumerate((ln_g, ln_b)):
                base = ((l * 4 + j) * 2 + gb) * 8
                lnp[:, base:base + 8] = arr[l, j].reshape(8, 128).T
    sh["lnp"] = lnp
    cw = np.asarray(inputs["conv_w"], np.float32)
    convw = np.zeros((128, L * 12), np.float32)
    for l in range(L):
        for tap in range(3):
            convw[:, (l * 3 + tap) * 4:(l * 3 + tap) * 4 + 4] = cw[l, tap].reshape(4, 128).T
    sh["convw"] = convw
    sh["gng"] = np.ascontiguousarray(np.asarray(inputs["gla_norm_g"], np.float32)[:L].T)
    gb = np.asarray(inputs["gla_gate_b"], np.float32)[:L]
    sh["gbias"] = np.ascontiguousarray(np.broadcast_to(gb.reshape(1, L * 512), (128, L * 512)))
    return sh


_NC_CACHE = {}


def kernel(**inputs):
    L = 4
    xp = np.asarray(inputs["x_prompt"], np.float32)
    xs = np.asarray(inputs["x_sample"], np.float32)
    mp = np.asarray(inputs["mem_prompt"], np.float32)
    ms = np.asarray(inputs["mem_sample"], np.float32)
    sh = prep_shared(inputs, L)
    in_maps = []
    for c in range(8):
        m = dict(sh)
        if c < 4:
            x = np.concatenate([xs[c], xp[c]], axis=0)
            mem = np.concatenate([ms[c], ms[c], mp[c]], axis=0)
            link = 1.0
        else:
            i0 = 4 + (c - 4) * 3
            x = np.concatenate([xp[i0], xp[i0 + 1], xp[i0 + 2]], axis=0)
            mem = np.concatenate([mp[i0], mp[i0 + 1], mp[i0 + 2]], axis=0)
            link = 0.0
        fl = np.zeros((128, 2), np.float32)
        fl[:, 1] = link
        m["x_in"] = np.ascontiguousarray(x)
        m["mem_in"] = np.ascontiguousarray(mem)
        m["flags"] = fl
        in_maps.append(m)
    if "nc" not in _NC_CACHE:
        _NC_CACHE["nc"] = build(L=L)
    nc = _NC_CACHE["nc"]
    res = run_bass_kernel_spmd(nc, in_maps, core_ids=list(range(8)))
    y_prompt = np.zeros_like(xp)
    y_sample = np.zeros_like(xs)
    for c in range(8):
        y = np.asarray(res.results[c]["y_out"], np.float32)
        if c < 4:
            y_sample[c] = y[0:4096]
            y_prompt[c] = y[4096:6144]
        else:
            i0 = 4 + (c - 4) * 3
            for s in range(3):
                y_prompt[i0 + s] = y[s * 2048:(s + 1) * 2048]
    return (y_prompt, y_sample)
```
